# Optimizing a Trainium2 kernel written in Bass

```python
import jax, jax.numpy as jnp
from jax import lax
import numpy as np

D_MODEL = 1024
BATCH = 4
SEQ = 4096
DEPTH = 4
DEC_BATCH = 32
DEC_SEQ = 8
PAST_LEN = 8192
PAGE_SIZE = 128

N_MIXERS = 2
N_RWKV = (DEPTH + 1) // 2
N_DSA = DEPTH // 2
NORM_EPS = 1e-6
D_FF = 4 * D_MODEL
RW_HEAD = 64
RW_HEADS = D_MODEL // RW_HEAD
LORA_DECAY = 64
LORA_AAA = 64
LORA_MV = 32
LORA_GATE = 160
GN_EPS = 64e-5
ATT_HEADS = 16
ATT_HEAD_DIM = 64
KV_HEADS = 4
GROUPS = ATT_HEADS // KV_HEADS
IDX_HEADS = 8
IDX_DIM = 64
TOPK_MAX = 256
QBLK = 128
ROPE_THETA = 500000.0
ROT_FRAC = 4
Q_W = ATT_HEADS * ATT_HEAD_DIM
KV_W = KV_HEADS * ATT_HEAD_DIM
IQ_W = IDX_HEADS * IDX_DIM
P_IN = Q_W + 2 * KV_W + IQ_W + IDX_DIM + IDX_HEADS
SPLITS = [Q_W, Q_W + KV_W, Q_W + 2 * KV_W, Q_W + 2 * KV_W + IQ_W, Q_W + 2 * KV_W + IQ_W + IDX_DIM]
F32 = jnp.float32

kernel_name = 'rwkv7_dsa_hybrid_step'


def rms_norm(x, g):
    xf = x.astype(F32)
    y = xf * lax.rsqrt(jnp.mean(xf * xf, axis=-1, keepdims=True) + NORM_EPS)
    return (y * g.astype(F32)).astype(x.dtype)


def ada_mods(c, w, b):
    m = jax.nn.silu(c) @ w + b
    return jnp.split(m[:, None, :], 6, axis=-1)


def rope_partial(x, pos):
    d = x.shape[-1]
    rd = d // ROT_FRAC
    half = rd // 2
    inv = ROPE_THETA ** (-jnp.arange(half, dtype=F32) * 2.0 / rd)
    ang = pos.astype(F32)[:, None] * inv[None, :]
    cos = jnp.cos(ang)[None, :, None, :]
    sin = jnp.sin(ang)[None, :, None, :]
    xf = x.astype(F32)
    x1 = xf[..., :half]
    x2 = xf[..., half:rd]
    out = jnp.concatenate([x1 * cos - x2 * sin, x2 * cos + x1 * sin, xf[..., rd:]], axis=-1)
    return out.astype(x.dtype)


def gather_rows(rows, idx):
    return jax.vmap(lambda r, i: r[i])(rows, idx)


def wkv7_scan(S0, r, w, k, v, a, b):
    def step(S, inp):
        r_t, w_t, k_t, v_t, a_t, b_t = inp
        sa = jnp.einsum('bhvk,bhk->bhv', S, a_t)
        S = S * w_t[:, :, None, :] + sa[..., None] * b_t[:, :, None, :] + v_t[..., None] * k_t[:, :, None, :]
        y = jnp.einsum('bhvk,bhk->bhv', S, r_t)
        return S, y
    xs = tuple(jnp.moveaxis(t.astype(F32), 1, 0) for t in (r, w, k, v, a, b))
    S, ys = lax.scan(step, S0.astype(F32), xs)
    return S, jnp.moveaxis(ys, 0, 1)


def rwkv7_mix(h, shift_prev, S0, v_first, P, j):
    B, T, D = h.shape
    hd = lambda t: t.reshape(B, T, RW_HEADS, RW_HEAD)
    mix = P['rw_mix'][j]
    h_prev = jnp.concatenate([shift_prev[:, None, :].astype(h.dtype), h[:, :-1]], axis=1)
    dx = h_prev - h
    xr, xw, xk, xv, xa, xg = [h + dx * mix[n] for n in range(6)]
    w_rkv = P['rw_w_rkv'][j]
    r = xr @ w_rkv[0]
    k = xk @ w_rkv[1]
    v = xv @ w_rkv[2]
    w_pre = (P['rw_w0'][j] + jnp.tanh(xw @ P['rw_w1'][j]) @ P['rw_w2'][j]).astype(F32)
    decay = jnp.exp(-jnp.exp(-jax.nn.softplus(-w_pre) - 0.5))
    if j == 0:
        v_first = v
    else:
        v = v + (v_first - v) * jax.nn.sigmoid(P['rw_v0'][j - 1] + (xv @ P['rw_v1'][j - 1]) @ P['rw_v2'][j - 1])
    a = jax.nn.sigmoid(P['rw_a0'][j] + (xa @ P['rw_a1'][j]) @ P['rw_a2'][j])
    g = jax.nn.sigmoid(xg @ P['rw_g1'][j]) @ P['rw_g2'][j]
    kk = hd(k * P['rw_k_k'][j]).astype(F32)
    kk = kk * lax.rsqrt(jnp.maximum(jnp.sum(kk * kk, axis=-1, keepdims=True), 1e-24))
    k = k * (1 + (a - 1) * P['rw_k_a'][j])
    S, y = wkv7_scan(S0, hd(r), hd(decay), hd(k), hd(v), -kk, kk * hd(a).astype(F32))
    mu = jnp.mean(y, axis=-1, keepdims=True)
    var = jnp.mean(jnp.square(y - mu), axis=-1, keepdims=True)
    y = ((y - mu) * lax.rsqrt(var + GN_EPS)).reshape(B, T, D)
    y = y * P['rw_lnx_w'][j].astype(F32) + P['rw_lnx_b'][j].astype(F32)
    bonus = jnp.sum(hd(r * k).astype(F32) * P['rw_r_k'][j].astype(F32), axis=-1, keepdims=True) * hd(v).astype(F32)
    y = (y + bonus.reshape(B, T, D)).astype(h.dtype)
    out = (y * g) @ P['rw_w_o'][j]
    return out, v_first, S, h[:, -1]


def dsa_project(h, pos, P, j):
    B, T, _ = h.shape
    q, k, v, qi, ki, wi = jnp.split(h @ P['att_w_in'][j], SPLITS, axis=-1)
    q = rope_partial(rms_norm(q.reshape(B, T, ATT_HEADS, ATT_HEAD_DIM), P['att_q_norm'][j]), pos)
    k = rope_partial(rms_norm(k.reshape(B, T, KV_HEADS, ATT_HEAD_DIM), P['att_k_norm'][j]), pos)
    v = v.reshape(B, T, KV_HEADS, ATT_HEAD_DIM)
    qi = rope_partial(qi.reshape(B, T, IDX_HEADS, IDX_DIM), pos)
    ki = rope_partial(rms_norm(ki, P['idx_k_norm'][j])[:, :, None, :], pos)[:, :, 0, :]
    wi = wi * IDX_HEADS ** -0.5
    return q, k, v, qi, ki, wi


def select_keys(qi, wi, ki, pos_q, topk):
    L = ki.shape[1]
    logits = jnp.einsum('bthd,bsd->bths', qi.astype(F32), ki.astype(F32)) * IDX_DIM ** -0.5
    score = jnp.einsum('bths,bth->bts', jax.nn.relu(logits), wi.astype(F32))
    pos_k = jnp.arange(L)
    adm = pos_k[None, None, :] <= pos_q[None, :, None]
    _, idx = lax.top_k(jnp.where(adm, score, -jnp.inf), topk)
    valid = idx <= pos_q[None, :, None]
    return idx, valid


def sparse_attend(q, kg, vg, valid):
    B, T = q.shape[:2]
    qg = q.reshape(B, T, KV_HEADS, GROUPS, ATT_HEAD_DIM).astype(F32)
    s = jnp.einsum('bthgd,btshd->bthgs', qg, kg.astype(F32)) * ATT_HEAD_DIM ** -0.5
    s = jnp.where(valid[:, :, None, None, :], s, -jnp.inf)
    p = jax.nn.softmax(s, axis=-1)
    o = jnp.einsum('bthgs,btshd->bthgd', p, vg.astype(F32))
    return o.reshape(B, T, Q_W).astype(q.dtype)


def dsa_prompt(q, k, v, qi, ki, wi):
    B, S = q.shape[:2]
    topk = min(TOPK_MAX, S // 4)
    def block(i):
        start = i * QBLK
        sl = lambda t: lax.dynamic_slice_in_dim(t, start, QBLK, axis=1)
        pos_q = start + jnp.arange(QBLK)
        idx, valid = select_keys(sl(qi), sl(wi), ki, pos_q, topk)
        return sparse_attend(sl(q), gather_rows(k, idx), gather_rows(v, idx), valid)
    o = lax.map(block, jnp.arange(S // QBLK))
    return jnp.moveaxis(o, 0, 1).reshape(B, S, Q_W)


def dsa_sample(q, k, v, qi, ki, wi, pool_k, pool_v, pool_ki, page_table):
    B, T = q.shape[:2]
    n_pages = PAST_LEN // PAGE_SIZE
    L = PAST_LEN + T
    topk = min(TOPK_MAX, L // 4)
    ki_past = pool_ki[page_table].reshape(B, PAST_LEN, IDX_DIM)
    ki_all = jnp.concatenate([ki_past, ki.astype(ki_past.dtype)], axis=1)
    pos_q = PAST_LEN + jnp.arange(T)
    idx, valid = select_keys(qi, wi, ki_all, pos_q, topk)
    in_past = (idx < PAST_LEN)[..., None, None]
    page = jnp.minimum(idx // PAGE_SIZE, n_pages - 1)
    phys = gather_rows(page_table, page) * PAGE_SIZE + idx % PAGE_SIZE
    loc = jnp.clip(idx - PAST_LEN, 0, T - 1)
    k_flat = pool_k.reshape(-1, KV_HEADS, ATT_HEAD_DIM)
    v_flat = pool_v.reshape(-1, KV_HEADS, ATT_HEAD_DIM)
    kg = jnp.where(in_past, k_flat[phys], gather_rows(k, loc))
    vg = jnp.where(in_past, v_flat[phys], gather_rows(v, loc))
    return sparse_attend(q, kg, vg, valid)


def run_trunk(x, c, pos, wkv0, shift0, attn_fn, P):
    v_first = None
    ks, vs, kis, wkvs, shifts = [], [], [], [], []
    for i in range(DEPTH):
        j = i // N_MIXERS
        sh_a, sc_a, g_a, sh_f, sc_f, g_f = ada_mods(c, P['w_ada'][i], P['b_ada'][i])
        h = rms_norm(x, P['norm_g'][i, 0]) * (1 + sc_a) + sh_a
        if i % N_MIXERS == 0:
            o, v_first, S, last = rwkv7_mix(h, shift0[j], wkv0[j], v_first, P, j)
            wkvs.append(S)
            shifts.append(last)
        else:
            q, k, v, qi, ki, wi = dsa_project(h, pos, P, j)
            o = attn_fn(j, q, k, v, qi, ki, wi) @ P['att_w_o'][j]
            ks.append(k)
            vs.append(v)
            kis.append(ki)
        x = x + g_a * o
        h = rms_norm(x, P['norm_g'][i, 1]) * (1 + sc_f) + sh_f
        u = jax.nn.relu(h @ P['w_up'][i])
        x = x + g_f * ((u * u) @ P['w_down'][i])
    return x, jnp.stack(ks), jnp.stack(vs), jnp.stack(kis), jnp.stack(wkvs), jnp.stack(shifts)


def setup_inputs(seed: int = 0) -> dict:
    key = jax.random.key(seed)
    ks = iter(jax.random.split(key, 48))
    D = D_MODEL
    def nrm(shape, scale):
        return scale * jax.random.normal(next(ks), shape, F32)
    n_pages = PAST_LEN // PAGE_SIZE
    in_use = DEC_BATCH * n_pages
    pool = in_use + (in_use + 3) // 4
    x_prompt = nrm((BATCH, SEQ, D), 1.0)
    x_sample = nrm((DEC_BATCH, DEC_SEQ, D), 1.0)
    cache_k = nrm((N_DSA, pool, PAGE_SIZE, KV_HEADS, ATT_HEAD_DIM), 1.0)
    cache_v = nrm((N_DSA, pool, PAGE_SIZE, KV_HEADS, ATT_HEAD_DIM), 1.0)
    cache_idx_k = nrm((N_DSA, pool, PAGE_SIZE, IDX_DIM), 1.0)
    state_wkv = nrm((N_RWKV, DEC_BATCH, RW_HEADS, RW_HEAD, RW_HEAD), 0.3)
    state_shift = nrm((N_RWKV, DEC_BATCH, D), 1.0)
    page_table = jax.random.permutation(next(ks), pool)[:in_use].reshape(DEC_BATCH, n_pages).astype(jnp.int32)
    c_prompt = nrm((BATCH, D), 1.0)
    c_sample = nrm((DEC_BATCH, D), 1.0)
    norm_g = 1.0 + nrm((DEPTH, 2, D), 0.05)
    w_ada = nrm((DEPTH, D, 6 * D), 0.5 * D ** -0.5)
    b_ada = nrm((DEPTH, 6 * D), 0.02)
    w_up = nrm((DEPTH, D, D_FF), D ** -0.5)
    w_down = nrm((DEPTH, D_FF, D), D_FF ** -0.5)
    rw_mix = jax.random.uniform(next(ks), (N_RWKV, 6, D), F32, 0.0, 1.0)
    rw_w_rkv = nrm((N_RWKV, 3, D, D), D ** -0.5)
    rw_w_o = nrm((N_RWKV, D, D), D ** -0.5)
    rw_w0 = jnp.linspace(-6.5, -1.5, D, dtype=F32)[None, :] + nrm((N_RWKV, D), 0.1)
    rw_w1 = nrm((N_RWKV, D, LORA_DECAY), D ** -0.5)
    rw_w2 = nrm((N_RWKV, LORA_DECAY, D), 0.3 * LORA_DECAY ** -0.5)
    rw_a0 = nrm((N_RWKV, D), 0.1)
    rw_a1 = nrm((N_RWKV, D, LORA_AAA), D ** -0.5)
    rw_a2 = nrm((N_RWKV, LORA_AAA, D), 0.5 * LORA_AAA ** -0.5)
    rw_v0 = 1.0 + nrm((N_RWKV - 1, D), 0.1)
    rw_v1 = nrm((N_RWKV - 1, D, LORA_MV), D ** -0.5)
    rw_v2 = nrm((N_RWKV - 1, LORA_MV, D), 0.5 * LORA_MV ** -0.5)
    rw_g1 = nrm((N_RWKV, D, LORA_GATE), D ** -0.5)
    rw_g2 = nrm((N_RWKV, LORA_GATE, D), LORA_GATE ** -0.5)
    rw_k_k = 0.85 + nrm((N_RWKV, D), 0.05)
    rw_k_a = 1.0 + nrm((N_RWKV, D), 0.05)
    rw_r_k = nrm((N_RWKV, RW_HEADS, RW_HEAD), 0.1)
    rw_lnx_w = 1.0 + nrm((N_RWKV, D), 0.05)
    rw_lnx_b = nrm((N_RWKV, D), 0.02)
    att_w_in = nrm((N_DSA, D, P_IN), D ** -0.5)
    att_w_o = nrm((N_DSA, Q_W, D), Q_W ** -0.5)
    att_q_norm = 1.0 + nrm((N_DSA, ATT_HEAD_DIM), 0.05)
    att_k_norm = 1.0 + nrm((N_DSA, ATT_HEAD_DIM), 0.05)
    idx_k_norm = 1.0 + nrm((N_DSA, IDX_DIM), 0.05)
    return {'x_prompt': x_prompt, 'x_sample': x_sample, 'cache_k': cache_k, 'cache_v': cache_v,
            'cache_idx_k': cache_idx_k, 'state_wkv': state_wkv, 'state_shift': state_shift,
            'page_table': page_table, 'c_prompt': c_prompt, 'c_sample': c_sample,
            'norm_g': norm_g, 'w_ada': w_ada, 'b_ada': b_ada, 'w_up': w_up, 'w_down': w_down,
            'rw_mix': rw_mix, 'rw_w_rkv': rw_w_rkv, 'rw_w_o': rw_w_o, 'rw_w0': rw_w0, 'rw_w1': rw_w1,
            'rw_w2': rw_w2, 'rw_a0': rw_a0, 'rw_a1': rw_a1, 'rw_a2': rw_a2, 'rw_v0': rw_v0,
            'rw_v1': rw_v1, 'rw_v2': rw_v2, 'rw_g1': rw_g1, 'rw_g2': rw_g2, 'rw_k_k': rw_k_k,
            'rw_k_a': rw_k_a, 'rw_r_k': rw_r_k, 'rw_lnx_w': rw_lnx_w, 'rw_lnx_b': rw_lnx_b,
            'att_w_in': att_w_in, 'att_w_o': att_w_o, 'att_q_norm': att_q_norm,
            'att_k_norm': att_k_norm, 'idx_k_norm': idx_k_norm}


def reference(x_prompt, x_sample, cache_k, cache_v, cache_idx_k, state_wkv, state_shift, page_table,
              c_prompt, c_sample, norm_g, w_ada, b_ada, w_up, w_down, rw_mix, rw_w_rkv, rw_w_o,
              rw_w0, rw_w1, rw_w2, rw_a0, rw_a1, rw_a2, rw_v0, rw_v1, rw_v2, rw_g1, rw_g2,
              rw_k_k, rw_k_a, rw_r_k, rw_lnx_w, rw_lnx_b, att_w_in, att_w_o, att_q_norm,
              att_k_norm, idx_k_norm):
    P = dict(norm_g=norm_g, w_ada=w_ada, b_ada=b_ada, w_up=w_up, w_down=w_down,
             rw_mix=rw_mix, rw_w_rkv=rw_w_rkv, rw_w_o=rw_w_o, rw_w0=rw_w0, rw_w1=rw_w1,
             rw_w2=rw_w2, rw_a0=rw_a0, rw_a1=rw_a1, rw_a2=rw_a2, rw_v0=rw_v0, rw_v1=rw_v1,
             rw_v2=rw_v2, rw_g1=rw_g1, rw_g2=rw_g2, rw_k_k=rw_k_k, rw_k_a=rw_k_a, rw_r_k=rw_r_k,
             rw_lnx_w=rw_lnx_w, rw_lnx_b=rw_lnx_b, att_w_in=att_w_in, att_w_o=att_w_o,
             att_q_norm=att_q_norm, att_k_norm=att_k_norm, idx_k_norm=idx_k_norm)

    def prompt_attn(j, q, k, v, qi, ki, wi):
        return dsa_prompt(q, k, v, qi, ki, wi)

    def sample_attn(j, q, k, v, qi, ki, wi):
        return dsa_sample(q, k, v, qi, ki, wi, cache_k[j], cache_v[j], cache_idx_k[j], page_table)

    wkv0 = jnp.zeros((N_RWKV, BATCH, RW_HEADS, RW_HEAD, RW_HEAD), F32)
    shift0 = jnp.zeros((N_RWKV, BATCH, D_MODEL), x_prompt.dtype)
    y_prompt, k_p, v_p, ki_p, wkv_p, shift_p = run_trunk(
        x_prompt, c_prompt, jnp.arange(SEQ), wkv0, shift0, prompt_attn, P)
    y_sample, k_s, v_s, ki_s, wkv_s, shift_s = run_trunk(
        x_sample, c_sample, PAST_LEN + jnp.arange(DEC_SEQ), state_wkv, state_shift, sample_attn, P)
    return (y_prompt, y_sample, k_p, v_p, ki_p, wkv_p, shift_p, k_s, v_s, ki_s, wkv_s, shift_s)
```

```python
import contextlib
import numpy as np
import concourse.bass as bass
import concourse.mybir as mybir
from concourse.bass_utils import run_bass_kernel_spmd

F32 = mybir.dt.float32
BF16 = mybir.dt.bfloat16
I32 = mybir.dt.int32
ALU = mybir.AluOpType
AF = mybir.ActivationFunctionType
AX = mybir.AxisListType

SEM_EPOCH = 30000
N_DMA_SEMS = 72


class Buf:
    __slots__ = ("name", "last_w", "readers", "dma_readers", "dsem", "dstage", "last_dma", "excl")

    def __init__(self, name=""):
        self.name = name
        self.excl = False
        self.last_w = None
        self.readers = {}
        self.dma_readers = []
        self.dsem = None
        self.dstage = -1
        self.last_dma = None


class Instr:
    __slots__ = ("eng", "idx", "fn", "deps", "needs_inc", "semval", "is_dma", "dsem", "dval",
                 "queue", "epoch", "stage")

    def __init__(self, eng, fn):
        self.eng = eng
        self.fn = fn
        self.deps = []
        self.needs_inc = False
        self.semval = None
        self.is_dma = False
        self.dsem = None
        self.dval = None
        self.queue = None
        self.epoch = 0
        self.stage = 0


class Rec:
    def __init__(self):
        self.calls = []

    def __getattr__(self, name):
        def f(*a, **k):
            self.calls.append((name, a, k))
            return self
        return f


def _replay(engh, calls):
    rr = None
    for (name, a, k) in calls:
        rr = getattr(engh, name)(*a, **k)
    return rr


class PSem:
    __slots__ = ("h", "count")

    def __init__(self, h):
        self.h = h
        self.count = 0


class Prog:
    ENGS = ("pe", "act", "dve", "pool", "sp")
    HANDLES = {"pe": "tensor", "act": "scalar", "dve": "vector", "pool": "gpsimd", "sp": "sync"}

    def __init__(self, nc, es, bar_src, bar_dst):
        self.nc = nc
        self.es = es
        self.stage = 0
        self.streams = {e: [] for e in self.ENGS}
        self.know = {e: {} for e in self.ENGS}
        self.know_dma = {e: {} for e in self.ENGS}
        self.cum = {e: 0 for e in self.ENGS}
        self.esems = {e: [] for e in self.ENGS}
        self.free_dsems = [PSem(es.enter_context(nc.semaphore("dq%d" % i))) for i in range(N_DMA_SEMS)]
        self.used_dsems = []
        self.stage_dmas = []
        self.bar_sem = PSem(es.enter_context(nc.semaphore("barrier")))
        self.bar_src, self.bar_dst = bar_src, bar_dst
        self.n_instr = 0
        self.disabled = False

    def _collect(self, eng, r, w):
        deps = []
        for b in r:
            if b.last_w is not None:
                deps.append(b.last_w)
            if b.excl:
                for e2, ins in b.readers.items():
                    if e2 != eng:
                        deps.append(ins)
        for b in w:
            if b.last_w is not None:
                deps.append(b.last_w)
            for e2, ins in b.readers.items():
                if e2 != eng:
                    deps.append(ins)
            deps.extend(b.dma_readers)
        return deps

    def _filter(self, eng, deps, same_eng_raw):
        out = []
        kn = self.know[eng]
        kd = self.know_dma[eng]
        best = {}
        for d in deps:
            if d.stage != self.stage:
                continue
            if d.is_dma:
                key = id(d.dsem)
                if kd.get(key, -1) >= d.dval:
                    continue
                kd[key] = d.dval
                out.append(d)
            else:
                if d.eng == eng and d not in same_eng_raw:
                    continue
                if kn.get(d.eng, -1) >= d.idx:
                    continue
                if d.eng not in best or best[d.eng].idx < d.idx:
                    best[d.eng] = d
        for e2, d in best.items():
            kn[e2] = d.idx
            d.needs_inc = True
            out.append(d)
        return out

    def op(self, eng, fn, r=(), w=()):
        if self.disabled:
            return None
        rec = Rec()
        fn(rec)
        ins = Instr(eng, rec.calls)
        ins.stage = self.stage
        ins.idx = len(self.streams[eng])
        deps = self._collect(eng, r, w)
        raw = set()
        for b in r:
            lw = b.last_w
            if lw is not None and lw.eng == eng and not lw.is_dma:
                raw.add(lw)
        ins.deps = self._filter(eng, deps, raw)
        self.streams[eng].append(ins)
        for b in r:
            b.readers[eng] = ins
        for b in w:
            b.last_w = ins
            b.readers = {}
            b.dma_readers = []
        self.n_instr += 1
        return ins

    def _buf_sem(self, b):
        if b.dstage != self.stage or b.dsem is None:
            if not self.free_dsems:
                raise RuntimeError("out of DMA semaphores in stage")
            b.dsem = self.free_dsems.pop()
            b.dstage = self.stage
            b.last_dma = None
            self.used_dsems.append(b.dsem)
        return b.dsem

    def dma(self, pairs, r=(), w=(), sync=None, queue="sp", fn=None):
        if self.disabled:
            return None
        if sync is None:
            sync = (w[0] if w else r[0])
        ps = self._buf_sem(sync)
        ins = Instr("dma", None)
        ins.stage = self.stage
        ins.is_dma = True
        ins.queue = queue
        ins.dsem = ps
        n = 1 if fn is not None else len(pairs)
        ps.count += 16 * n
        ins.dval = ps.count
        deps = self._collect("dma", r, w)
        if sync.last_dma is not None:
            deps.append(sync.last_dma)
        sync.last_dma = ins
        ins.deps = self._filter(queue, deps, set())
        if fn is not None:
            rec = Rec()
            fn(rec)
            fn = rec.calls
        ins.fn = (pairs, fn)
        ins.idx = len(self.streams[queue])
        self.streams[queue].append(ins)
        self.stage_dmas.append(ins)
        for b in r:
            b.dma_readers.append(ins)
        for b in w:
            b.last_w = ins
            b.readers = {}
            b.dma_readers = []
        self.n_instr += 1
        return ins

    def flush(self):
        nc = self.nc
        lasts = []
        for e in ("pe", "act", "dve", "pool"):
            comp = [i for i in self.streams[e] if not i.is_dma]
            if comp:
                comp[-1].needs_inc = True
                lasts.append(comp[-1])
        dma_final = {}
        for d in self.stage_dmas:
            dma_final[id(d.dsem)] = (d.dsem, max(d.dval, dma_final.get(id(d.dsem), (None, 0))[1]))
        self.bar_sem.count += 16
        bar_val = self.bar_sem.count
        for e in self.ENGS:
            for ins in self.streams[e]:
                if ins.is_dma:
                    continue
                if ins.needs_inc:
                    c = self.cum[e]
                    ins.epoch = c // SEM_EPOCH
                    ins.semval = c % SEM_EPOCH + 1
                    self.cum[e] = c + 1
            nep = (self.cum[e] + SEM_EPOCH - 1) // SEM_EPOCH
            while len(self.esems[e]) < nep:
                self.esems[e].append(self.es.enter_context(nc.semaphore("s_%s_%d" % (e, len(self.esems[e])))))
        import os
        if os.environ.get("DEBUG_DUMP") == str(self.stage):
            for e in self.ENGS:
                print("==== engine", e, len(self.streams[e]))
                for ins in self.streams[e][-14:]:
                    nm = [c[0] for c in ins.fn] if not ins.is_dma else ("DMA", ins.dval)
                    print("  ", ins.idx, nm, "inc" if ins.needs_inc else "", ins.semval,
                          "waits:", [(d.eng, d.idx, d.semval if not d.is_dma else d.dval) for d in ins.deps])
        esems = self.esems
        streams = self.streams
        bar_sem = self.bar_sem
        bar_src, bar_dst = self.bar_src, self.bar_dst

        with nc.Block() as block:
            def make(e):
                def body(engh):
                    for ins in streams[e]:
                        for d in ins.deps:
                            if d.is_dma:
                                engh.wait_ge(d.dsem.h, d.dval)
                            else:
                                engh.wait_ge(esems[d.eng][d.epoch], d.semval)
                        if ins.is_dma:
                            pairs, fn = ins.fn
                            if fn is not None:
                                _replay(engh, fn).then_inc(ins.dsem.h, 16)
                            else:
                                for (o, i_) in pairs:
                                    engh.dma_start(out=o, in_=i_).then_inc(ins.dsem.h, 16)
                        else:
                            rr = _replay(engh, ins.fn)
                            if ins.needs_inc:
                                rr.then_inc(esems[e][ins.epoch], 1)
                    if e == "sp":
                        for l in lasts:
                            engh.wait_ge(esems[l.eng][l.epoch], l.semval)
                        for (s, v) in dma_final.values():
                            engh.wait_ge(s.h, v)
                        engh.dma_start(out=bar_dst, in_=bar_src).then_inc(bar_sem.h, 16)
                    engh.wait_ge(bar_sem.h, bar_val)
                return body

            for e in self.ENGS:
                getattr(block, self.HANDLES[e])(make(e))
        self.stage += 1
        self.streams = {e: [] for e in self.ENGS}
        self.know = {e: {} for e in self.ENGS}
        self.know_dma = {e: {} for e in self.ENGS}
        self.free_dsems.extend(self.used_dsems)
        self.used_dsems = []
        self.stage_dmas = []


D = 1024
SEQ = 4096
NSMP = 4
TS = 8
NTOK = SEQ + NSMP * TS
DFF = 4096
NPAGE = 64
POOLN = 2560
PIN2 = 2184
C_Q, C_K, C_V, C_QI, C_KI, C_WI = 0, 1024, 1280, 1536, 2048, 2176
NEG = -1.0e30
GN_EPS = 64e-5
NORM_EPS = 1e-6


class T:
    def __init__(self, t, name, nsub=0):
        self.t = t
        self.b = Buf(name)
        self.bs = [Buf(name + str(i)) for i in range(nsub)]


class KB:
    def __init__(self):
        self.nc = bass.Bass("TRN2", target_bir_lowering=False)
        self.es = contextlib.ExitStack()
        self.din = {}
        self.dout = {}
        self.uid = 0

    def inp(self, name, shape, dt=F32):
        self.din[name] = self.nc.dram_tensor(name, list(shape), dt, kind="ExternalInput").ap()
        return self.din[name]

    def outp(self, name, shape, dt=F32):
        self.dout[name] = self.nc.dram_tensor(name, list(shape), dt, kind="ExternalOutput").ap()
        return self.dout[name]

    def scratch(self, name, shape, dt=F32):
        return self.nc.dram_tensor(name, list(shape), dt).ap()

    def sb(self, st, name, shape, dt, nsub=0):
        self.uid += 1
        nm = "%s_%d" % (name, self.uid)
        return T(st.enter_context(self.nc.sbuf_tensor(nm, list(shape), dt)), nm, nsub)

    def ps(self, st, name, shape, dt):
        self.uid += 1
        nm = "%s_%d" % (name, self.uid)
        t = T(st.enter_context(self.nc.psum_tensor(nm, list(shape), dt)), nm)
        t.b.excl = True
        return t


class RR:
    def __init__(self, tiles):
        self.tiles = tiles
        self.i = 0

    def __call__(self):
        t = self.tiles[self.i % len(self.tiles)]
        self.i += 1
        return t


def build_program():
    kb = KB()
    nc = kb.nc
    es = kb.es
    xT_in = kb.inp("xT", [D, NTOK])
    cT_in = kb.inp("cT", [D, 5])
    w_ada = kb.inp("w_ada", [4, D, 6 * D])
    b_adaT = kb.inp("b_adaT", [4, 128, 48])
    norm_gT = kb.inp("norm_gT", [4, 2, 128, 8])
    w_up = kb.inp("w_up", [4, D, DFF])
    w_down = kb.inp("w_down", [4, DFF, D])
    rw_w_rkv = kb.inp("rw_w_rkv", [2, 3, D, D])
    rw_w_o = kb.inp("rw_w_o", [2, D, D])
    rw_w1 = kb.inp("rw_w1", [2, D, 64])
    rw_w2 = kb.inp("rw_w2", [2, 64, D])
    rw_a1 = kb.inp("rw_a1", [2, D, 64])
    rw_a2 = kb.inp("rw_a2", [2, 64, D])
    rw_v1 = kb.inp("rw_v1", [1, D, 32])
    rw_v2 = kb.inp("rw_v2", [1, 32, D])
    rw_g1 = kb.inp("rw_g1", [2, D, 160])
    rw_g2 = kb.inp("rw_g2", [2, 160, D])
    rwvecT = kb.inp("rwvecT", [2, 128, 12, 8])
    rwrows = kb.inp("rwrows", [2, 3, D])
    att_w_in = kb.inp("att_w_in", [2, D, PIN2])
    att_w_o = kb.inp("att_w_o", [2, D, D])
    dsavec = kb.inp("dsavec", [2, 128, 3])
    cache_k = [kb.inp("cache_k%d" % j_, [POOLN * 16, 2048]) for j_ in range(2)]
    cache_v = [kb.inp("cache_v%d" % j_, [POOLN * 16, 2048]) for j_ in range(2)]
    cache_ik = [kb.inp("cache_ik%d" % j_, [POOLN * 4, 2048]) for j_ in range(2)]
    ptT = kb.inp("ptT", [NPAGE, NSMP], I32)
    wkvT0 = kb.inp("wkvT0", [2, NSMP, 128, 8, 64])
    shiftT0 = kb.inp("shiftT0", [2, 128, 8, NSMP])
    cst_sq = kb.inp("cst_sq", [8, 128, 128])
    cst_msi = kb.inp("cst_msi", [128, 512])
    cst_msi8 = kb.inp("cst_msi8", [8, 32])
    cst_ml8 = kb.inp("cst_ml8", [8, 16])
    cst_msi32 = kb.inp("cst_msi32", [32, 128])
    cst_ml32 = kb.inp("cst_ml32", [32, 64])
    cst_misc = kb.inp("cst_misc", [128, 64])
    ropecs = kb.inp("ropecs", [2, 128, NTOK])

    yT_o = kb.outp("yT", [D, NTOK])
    kT_o = kb.outp("kT", [2, 256, NTOK])
    v_o = kb.outp("v", [2, NTOK, 256])
    kiT_o = kb.outp("kiT", [2, 64, NTOK])
    wkv_o = kb.outp("wkv", [2, 5, 128, 8, 64])
    shift_o = kb.outp("shift", [2, 128, 8, 5])

    xs = kb.scratch("xs", [D, NTOK])
    vfirst = kb.scratch("vfirst", [NTOK, D])
    qs = kb.scratch("qs", [D, NTOK], BF16)
    qis = kb.scratch("qis", [512, NTOK], BF16)
    wsc = kb.scratch("wsc", [NSMP, 64, 1])
    ms = kb.scratch("ms", [D, NTOK], BF16)
    bar_a = kb.scratch("bar_a", [1, 16])
    bar_b = kb.scratch("bar_b", [1, 16])

    P = Prog(nc, es, bar_a, bar_b)
    OP = P.op
    DMA = P.dma

    xs_b = [Buf("xs%d" % i) for i in range(9)]
    vf_b = [Buf("vf%d" % i) for i in range(9)]
    qs_b = [Buf("qs%d" % i) for i in range(9)]
    qis_b = [Buf("qis%d" % i) for i in range(9)]
    wsc_b = Buf("wsc")
    ms_b = [Buf("ms%d" % i) for i in range(9)]
    out_b = Buf("outs")

    def tile_of(col):
        return min(col // 512, 8)

    top = es
    ident_bf = kb.sb(top, "ident_bf", [128, 128], BF16)
    ident_f = kb.sb(top, "ident_f", [128, 128], F32)
    ones_bf = kb.sb(top, "ones_bf", [128, 128], BF16)
    blk64_bf = kb.sb(top, "blk64_bf", [128, 128], BF16)
    ropeRT_bf = kb.sb(top, "ropeRT", [128, 128], BF16)
    sel_bf = [kb.sb(top, "sel%d" % i, [128, 128], BF16) for i in range(2)]
    tri_f = kb.sb(top, "tri", [128, 128], F32)
    maskL_bf = kb.sb(top, "maskL", [128, 256], BF16)
    msi_bf = kb.sb(top, "msi", [128, 512], BF16)
    msi8_bf = kb.sb(top, "msi8", [8, 32], BF16)
    ml8_bf = kb.sb(top, "ml8", [8, 16], BF16)
    msi32_bf = kb.sb(top, "msi32", [32, 128], BF16)
    ml32_bf = kb.sb(top, "ml32", [32, 64], BF16)
    misc_f = kb.sb(top, "misc", [128, 64], F32)
    blk2_bf = kb.sb(top, "blk2", [128, 2], BF16)
    eps_n = kb.sb(top, "eps_n", [128, 1], F32)
    eps_g = kb.sb(top, "eps_g", [128, 1], F32)
    zeros_f = kb.sb(top, "zeros", [128, 128], F32)
    mods = [kb.sb(top, "mods%d" % i, [128, 48, 5], F32) for i in range(4)]
    psc = [[kb.sb(top, "psc%d_%d" % (i, w), [128, 8, 5], F32) for w in range(2)] for i in range(4)]
    CONST = [ident_bf.b, ones_bf.b, blk64_bf.b, ropeRT_bf.b, sel_bf[0].b, sel_bf[1].b, tri_f.b, maskL_bf.b,
             msi_bf.b, msi8_bf.b, ml8_bf.b, misc_f.b, blk2_bf.b, eps_n.b, eps_g.b, zeros_f.b]

    def cload(dst, src, queue="pool"):
        DMA([(dst.t[:], src)], w=[dst.b], queue=queue)

    with contextlib.ExitStack() as st:
        cload(ident_bf, cst_sq[0]); cload(ident_f, cst_sq[0], queue="sp"); cload(ones_bf, cst_sq[1]); cload(blk64_bf, cst_sq[2]); cload(ropeRT_bf, cst_sq[3])
        cload(sel_bf[0], cst_sq[4]); cload(sel_bf[1], cst_sq[5]); cload(tri_f, cst_sq[6], queue="sp")
        DMA([(maskL_bf.t[:, 0:128], cst_sq[7]), (maskL_bf.t[:, 128:256], cst_sq[7])], w=[maskL_bf.b], queue="pool")
        cload(msi_bf, cst_msi); cload(msi8_bf, cst_msi8); cload(ml8_bf, cst_ml8); cload(msi32_bf, cst_msi32); cload(ml32_bf, cst_ml32); cload(misc_f, cst_misc, queue="sp")
        DMA([(blk2_bf.t[:], cst_misc[:, 0:2])], w=[blk2_bf.b], queue="pool")
        OP("dve", lambda e: e.memset(eps_n.t[:], NORM_EPS), w=[eps_n.b])
        OP("dve", lambda e: e.memset(eps_g.t[:], GN_EPS), w=[eps_g.b])
        OP("dve", lambda e: e.memset(zeros_f.t[:], 0.0), w=[zeros_f.b])
        cT = kb.sb(st, "cT", [128, 8, 5], F32)
        sc = kb.sb(st, "silu_c", [128, 8, 5], F32)
        DMA([(cT.t[:], cT_in.rearrange("(dc p) s -> p dc s", p=128))], w=[cT.b])
        OP("act", lambda e: e.activation(out=sc.t[:], in_=cT.t[:], func=AF.Silu), r=[cT.b], w=[sc.b])
        wa = RR([kb.sb(st, "wa%d" % i, [128, 8, 512], F32) for i in range(3)])
        psm = kb.ps(st, "psm", [128, 240], F32)
        badd = kb.sb(st, "badd", [128, 48], F32)
        ngt = kb.sb(st, "ngt", [128, 8], F32)
        for i in range(4):
            for g in range(12):
                w = wa()
                DMA([(w.t[:], w_ada[i].rearrange("(dc p) f -> p dc f", p=128)[:, :, g * 512:(g + 1) * 512])], w=[w.b])
                for cc in range(4):
                    col = (g * 4 + cc) * 5
                    for dc in range(8):
                        OP("pe", lambda e, w=w, cc=cc, dc=dc, col=col: e.matmul(
                            psm.t[:, col:col + 5], lhsT=w.t[:, dc, cc * 128:(cc + 1) * 128], rhs=sc.t[:, dc, :],
                            start=(dc == 0), stop=(dc == 7)), r=[w.b, sc.b], w=[psm.b])
            DMA([(badd.t[:], b_adaT[i])], w=[badd.b])
            m = mods[i]
            OP("dve", lambda e, m=m: e.tensor_tensor(
                out=m.t[:], in0=psm.t[:].rearrange("p (c s) -> p c s", s=5),
                in1=badd.t[:].rearrange("p (c o) -> p c o", o=1).to_broadcast([128, 48, 5]), op=ALU.add),
               r=[psm.b, badd.b], w=[m.b])
            for wch in range(2):
                pt_ = psc[i][wch]
                DMA([(ngt.t[:], norm_gT[i, wch])], w=[ngt.b])
                c0 = 8 if wch == 0 else 32
                OP("dve", lambda e, m=m, pt_=pt_, c0=c0: e.tensor_scalar(
                    out=pt_.t[:], in0=m.t[:, c0:c0 + 8, :], scalar1=1.0, scalar2=None, op0=ALU.add),
                   r=[m.b], w=[pt_.b])
                OP("dve", lambda e, pt_=pt_: e.tensor_tensor(
                    out=pt_.t[:], in0=pt_.t[:],
                    in1=ngt.t[:].rearrange("p (c o) -> p c o", o=1).to_broadcast([128, 8, 5]), op=ALU.mult),
                   r=[pt_.b, ngt.b], w=[pt_.b])
        P.flush()

    def MOD(i, which):
        m = mods[i]
        if which == "a":
            return psc[i][0], (m, 0), (m, 16)
        return psc[i][1], (m, 24), (m, 40)

    def emit_norm(st_tiles, xt, W, segs, i, which, out, ps_bank):
        pscale, (mt, shb), _ = MOD(i, which)
        sq, rst, xns = st_tiles
        OP("act", lambda e: e.activation(out=sq.t[:, :, 0:W], in_=xt.t[:, :, 0:W], func=AF.Square), r=[xt.b], w=[sq.b])
        bank = ps_bank()
        for dc in range(8):
            OP("pe", lambda e, dc=dc: e.matmul(bank.t[:, 0:W], lhsT=ones_bf.t[:], rhs=sq.t[:, dc, 0:W],
                                               start=(dc == 0), stop=(dc == 7)), r=[ones_bf.b, sq.b], w=[bank.b])
        OP("act", lambda e: e.activation(out=rst.t[:, 0:W], in_=bank.t[:, 0:W], func=AF.Sqrt, scale=1.0 / D,
                                         bias=eps_n.t[:, 0:1]), r=[bank.b, eps_n.b], w=[rst.b])
        OP("dve", lambda e: e.reciprocal(out=rst.t[:, 0:W], in_=rst.t[:, 0:W]), r=[rst.b], w=[rst.b])
        for dc in range(8):
            xn = xns()
            OP("dve", lambda e, dc=dc: e.tensor_tensor(out=xn.t[:, 0:W], in0=xt.t[:, dc, 0:W], in1=rst.t[:, 0:W],
                                                       op=ALU.mult), r=[xt.b, rst.b], w=[xn.b])
            for (c0, c1, s) in segs:
                OP("act", lambda e, dc=dc, c0=c0, c1=c1, s=s: e.activation(
                    out=out.t[:, dc, c0:c1], in_=xn.t[:, c0:c1], func=AF.Identity,
                    scale=pscale.t[:, dc, s:s + 1], bias=mt.t[:, shb + dc, s:s + 1]),
                   r=[xn.b, pscale.b, mt.b], w=[out.b])

    def segs_of(col0, W):
        if col0 < SEQ:
            return [(0, W, 0)]
        return [(TS * s, TS * s + TS, 1 + s) for s in range(NSMP)]

    def x_view(ap, col0, W):
        return ap.rearrange("(dc p) t -> p dc t", p=128)[:, :, col0:col0 + W]

    TILES512 = [(i * 512, 512) for i in range(8)] + [(SEQ, NSMP * TS)]

    def ffn_stage(i, dst, wo_src=None):
        gate_m = mods[i]
        with contextlib.ExitStack() as st:
            xts = RR([kb.sb(st, "fx%d" % k, [128, 8, 512], F32) for k in range(2)])
            ht = kb.sb(st, "fh", [128, 8, 512], BF16)
            sq = kb.sb(st, "fsq", [128, 8, 512], BF16)
            rst = kb.sb(st, "frst", [128, 512], F32)
            xns = RR([kb.sb(st, "fxn%d" % k, [128, 512], F32) for k in range(2)])
            u = kb.sb(st, "fu", [128, 32, 512], BF16, nsub=32)
            wus = RR([kb.sb(st, "wu%d" % k, [128, 8, 512], BF16) for k in range(2)])
            wds = RR([kb.sb(st, "wd%d" % k, [128, 32, 128], BF16) for k in range(2)])
            rts = RR([kb.sb(st, "fr%d" % k, [128, 512], F32) for k in range(3)])
            banks = RR([kb.ps(st, "fps%d" % k, [128, 512], F32) for k in range(6)])
            if wo_src is not None:
                wo = kb.sb(st, "wo", [128, 8, 1024], BF16)
                DMA([(wo.t[:], wo_src.rearrange("(dc p) f -> p dc f", p=128))], w=[wo.b], queue="pool")
                mts = RR([kb.sb(st, "fm%d" % k, [128, 8, 512], BF16) for k in range(2)])
            for (col0, W) in TILES512:
                ti = tile_of(col0)
                xt = xts()
                sg = segs_of(col0, W)
                src = xT_in if (i == 0) else xs
                DMA([(xt.t[:, :, 0:W], x_view(src, col0, W))], r=[xs_b[ti]], w=[xt.b])
                if wo_src is not None:
                    mt = mts()
                    DMA([(mt.t[:, :, 0:W], x_view(ms, col0, W))], r=[ms_b[ti]], w=[mt.b])
                    for oc in range(8):
                        bank = banks()
                        for c in range(8):
                            OP("pe", lambda e: e.matmul(bank.t[:, 0:W], lhsT=wo.t[:, c, oc * 128:(oc + 1) * 128],
                                                        rhs=mt.t[:, c, 0:W], start=(c == 0), stop=(c == 7)),
                               r=[wo.b, mt.b], w=[bank.b])
                        for (c0, c1, s_) in sg:
                            OP("dve", lambda e: e.scalar_tensor_tensor(
                                out=xt.t[:, oc, c0:c1], in0=bank.t[:, c0:c1], scalar=gate_m.t[:, 16 + oc, s_:s_ + 1],
                                in1=xt.t[:, oc, c0:c1], op0=ALU.mult, op1=ALU.add),
                               r=[bank.b, gate_m.b, xt.b], w=[xt.b])
                emit_norm((sq, rst, xns), xt, W, sg, i, "f", ht, banks)
                for g in range(8):
                    wu = wus()
                    DMA([(wu.t[:], w_up[i].rearrange("(dc p) f -> p dc f", p=128)[:, :, g * 512:(g + 1) * 512])],
                        w=[wu.b], queue="pool")
                    for fc in range(4):
                        bank = banks()
                        for dc in range(8):
                            OP("pe", lambda e: e.matmul(
                                bank.t[:, 0:W], lhsT=wu.t[:, dc, fc * 128:(fc + 1) * 128], rhs=ht.t[:, dc, 0:W],
                                start=(dc == 0), stop=(dc == 7)), r=[wu.b, ht.b], w=[bank.b])
                        rt = rts()
                        OP("act", lambda e: e.activation(out=rt.t[:, 0:W], in_=bank.t[:, 0:W], func=AF.Relu),
                           r=[bank.b], w=[rt.b])
                        ff = g * 4 + fc
                        OP("pool", lambda e: e.tensor_tensor(out=u.t[:, ff, 0:W], in0=rt.t[:, 0:W],
                                                             in1=rt.t[:, 0:W], op=ALU.mult),
                           r=[rt.b], w=[u.bs[ff]])
                for dc in range(8):
                    wd = wds()
                    DMA([(wd.t[:], w_down[i].rearrange("(fc p) d -> p fc d", p=128)[:, :, dc * 128:(dc + 1) * 128])],
                        w=[wd.b], queue="pool")
                    bank = banks()
                    for fc in range(32):
                        OP("pe", lambda e: e.matmul(
                            bank.t[:, 0:W], lhsT=wd.t[:, fc, :], rhs=u.t[:, fc, 0:W],
                            start=(fc == 0), stop=(fc == 31)), r=[wd.b, u.bs[fc]], w=[bank.b])
                    for (c0, c1, s_) in sg:
                        OP("dve", lambda e: e.scalar_tensor_tensor(
                            out=xt.t[:, dc, c0:c1], in0=bank.t[:, c0:c1], scalar=gate_m.t[:, 40 + dc, s_:s_ + 1],
                            in1=xt.t[:, dc, c0:c1], op0=ALU.mult, op1=ALU.add),
                           r=[bank.b, gate_m.b, xt.b], w=[xt.b])
                if dst is xs:
                    DMA([(x_view(xs, col0, W), xt.t[:, :, 0:W])], r=[xt.b], w=[xs_b[ti]], sync=xt.b)
                else:
                    DMA([(x_view(dst, col0, W), xt.t[:, :, 0:W])], r=[xt.b], w=[out_b], sync=xt.b)
            P.flush()

    def rwkv_stage(i):
        j = i // 2
        src = xT_in if i == 0 else xs
        with contextlib.ExitStack() as st:
            def wload(name, shape, src_ap, dt=BF16, queue="pool"):
                t = kb.sb(st, name, shape, dt)
                DMA([(t.t[:], src_ap)], w=[t.b], queue=queue)
                return t
            chunked = lambda ap: ap.rearrange("(dc p) f -> p dc f", p=128)
            Wr = wload("Wr", [128, 8, 1024], chunked(rw_w_rkv[j, 0]))
            Wk = wload("Wk", [128, 8, 1024], chunked(rw_w_rkv[j, 1]))
            Wv = wload("Wv", [128, 8, 1024], chunked(rw_w_rkv[j, 2]))
            w1 = wload("w1", [128, 8, 64], chunked(rw_w1[j]))
            a1 = wload("a1", [128, 8, 64], chunked(rw_a1[j]))
            g1 = wload("g1", [128, 8, 160], chunked(rw_g1[j]))
            w2 = wload("w2", [64, 1024], rw_w2[j])
            a2 = wload("a2", [64, 1024], rw_a2[j])
            g2a = wload("g2a", [128, 1024], rw_g2[j, 0:128, :])
            g2b = wload("g2b", [32, 1024], rw_g2[j, 128:160, :])
            if j >= 1:
                v1 = wload("v1", [128, 8, 32], chunked(rw_v1[j - 1]))
                v2 = wload("v2", [32, 1024], rw_v2[j - 1])
                v0row = wload("v0row", [128, 1024], rwrows[j, 2:3, :].partition_broadcast(128), F32, "sp")
            vec = wload("vec", [128, 12, 8], rwvecT[j], F32, "sp")
            lnw = wload("lnw", [128, 1024], rwrows[j, 0:1, :].partition_broadcast(128), F32, "sp")
            lnb = wload("lnb", [128, 1024], rwrows[j, 1:2, :].partition_broadcast(128), F32, "sp")
            sst = wload("sst", [128, 8, NSMP], shiftT0[j], F32, "sp")
            omk = kb.sb(st, "omk", [128, 8], F32)
            OP("dve", lambda e: e.tensor_scalar(out=omk.t[:], in0=vec.t[:, 9, :], scalar1=-1.0, scalar2=1.0,
                                                op0=ALU.mult, op1=ALU.add), r=[vec.b], w=[omk.b])
            Sfp = kb.sb(st, "Sfp", [128, 8, 64], F32, nsub=8)
            Sfs = Sfp
            Sbp = kb.sb(st, "Sbp", [128, 8, 128], F32, nsub=8)
            Sbs = Sbp
            hlast = kb.sb(st, "hlast", [128, 8, 1], F32)
            sh_out = kb.sb(st, "sh_out", [128, 8, 5], F32)
            OP("pool", lambda e: e.memset(Sfp.t[:], 0.0), w=Sfp.bs)
            OP("pool", lambda e: e.memset(Sbp.t[:], 0.0), w=Sbp.bs)
            OP("pool", lambda e: e.memset(hlast.t[:], 0.0), w=[hlast.b])
            F = lambda nm: kb.sb(st, nm, [128, 8, 128], F32)
            xt = F("xt"); h = F("h"); dx = F("dx"); r_f = F("r_f"); k_f = F("k_f"); w_f = F("w_f"); a_f = F("a_f")
            rs = F("rs"); Pex = F("Pex"); Pin = F("Pin")
            kk = h; kp = dx; bb = a_f; wsh = k_f
            sq = kb.sb(st, "sq", [128, 8, 128], BF16)
            rst = kb.sb(st, "rst", [128, 128], F32)
            xns = RR([kb.sb(st, "xn%d" % k, [128, 128], F32) for k in range(2)])
            xm = kb.sb(st, "xm", [128, 8, 128], BF16)
            xmv = kb.sb(st, "xmv", [128, 8, 128], BF16)
            rkp = kb.sb(st, "rkp", [128, 8, 128], BF16)
            t1b = kb.sb(st, "t1b", [64, 128], BF16)
            t4b = kb.sb(st, "t4b", [32, 128], BF16)
            sgA = kb.sb(st, "sgA", [128, 128], BF16)
            sgB = kb.sb(st, "sgB", [32, 128], BF16)
            tmpw = RR([kb.sb(st, "tmpw%d" % k, [128, 128], F32) for k in range(2)])
            vtok = kb.sb(st, "vtok", [128, 1024], F32)
            vb = vtok
            yf = kb.sb(st, "yf", [128, 1024], F32)
            tmpv = kb.sb(st, "tmpv", [128, 1024], F32)
            vft = yf; gt = tmpv
            yg = kb.sb(st, "yg", [128, 1024], BF16)
            ygT = kb.sb(st, "ygT", [128, 8, 128], BF16)
            AR = kb.sb(st, "AR", [128, 8, 256], F32)
            Kt = kb.sb(st, "Kt", [128, 8, 128], F32); Bt = kb.sb(st, "Bt", [128, 8, 128], F32)
            BtZs = [RR([kb.sb(st, "BtZ%d_%d" % (hb_, k), [128, 128], F32) for k in range(2)]) for hb_ in range(2)]
            KtZs = [RR([kb.sb(st, "KtZ%d_%d" % (hb_, k), [128, 128], F32) for k in range(2)]) for hb_ in range(2)]
            KhT = kb.sb(st, "KhT", [128, 1024], F32); BhT = kb.sb(st, "BhT", [128, 1024], F32)
            mAs = RR([kb.sb(st, "mA%d" % k, [128, 512], F32) for k in range(2)])
            mBs = RR([kb.sb(st, "mB%d" % k, [128, 512], F32) for k in range(2)])
            l0s = RR([kb.sb(st, "l0%d" % k, [128, 256], F32) for k in range(2)])
            lvs = RR([kb.sb(st, "lv%d" % k, [128, 512], F32) for k in range(3)])
            l0rs = RR([kb.sb(st, "l0r%d" % k, [128, 256], F32) for k in range(2)])
            Xfs = RR([kb.sb(st, "Xf%d" % k, [128, 128], F32) for k in range(2)])
            st16 = [kb.sb(st, "st16_%d" % k, [128, 16], F32) for k in range(6)]
            s1, s2, mean, msq, rstdg, rk = st16
            yb = [kb.ps(st, "yb%d" % k, [128, 512], F32) for k in range(2)]
            psb1 = kb.ps(st, "psb1", [128, 512], F32)
            psb3 = kb.ps(st, "psb3", [128, 512], BF16)
            banks = RR([kb.ps(st, "rps%d" % k, [128, 512], F32) for k in range(4)])
            def mix(n, dst):
                for dc in range(8):
                    OP("dve", lambda e: e.scalar_tensor_tensor(out=dst.t[:, dc, 0:W], in0=dx.t[:, dc, 0:W],
                                                               scalar=vec.t[:, n, dc:dc + 1], in1=h.t[:, dc, 0:W],
                                                               op0=ALU.mult, op1=ALU.add),
                       r=[dx.b, vec.b, h.b], w=[dst.b])

            def proj_small(wt, ncol, xsrc, c0=0):
                bank = banks()
                for dc in range(8):
                    OP("pe", lambda e: e.matmul(bank.t[0:ncol, 0:W], lhsT=wt.t[:, dc, c0:c0 + ncol], rhs=xsrc.t[:, dc, 0:W],
                                                start=(dc == 0), stop=(dc == 7)), r=[wt.b, xsrc.b], w=[bank.b])
                return bank

            import os as _os
            _nt = int(_os.environ.get("RW_TILES", "32"))
            _ph = float(_os.environ.get("RW_PHASE", "99"))

            def chk(k):
                if k > _ph:
                    P.disabled = True
            tiles = [(128 * t_, 128, 32, 4) for t_ in range(_nt)] + ([(SEQ, NSMP * TS, TS, NSMP)] if _os.environ.get("RW_SAMPLE", "1") == "1" else [])
            for (col0, W, C, nch) in tiles:
                ti = tile_of(col0)
                is_s = col0 >= SEQ
                sg = segs_of(col0, W)
                DMA([(xt.t[:, :, 0:W], x_view(src, col0, W))], r=[xs_b[ti]], w=[xt.b])
                emit_norm((sq, rst, xns), xt, W, sg, i, "a", h, banks)
                chk(1)
                for (c0, c1, s_) in sg:
                    prev = sst.t[:, :, s_ - 1:s_] if is_s else hlast.t[:, :, 0:1]
                    pb = sst.b if is_s else hlast.b
                    OP("dve", lambda e: e.tensor_tensor(out=dx.t[:, :, c0 + 1:c1], in0=h.t[:, :, c0:c1 - 1],
                                                        in1=h.t[:, :, c0 + 1:c1], op=ALU.subtract), r=[h.b], w=[dx.b])
                    OP("dve", lambda e: e.tensor_tensor(out=dx.t[:, :, c0:c0 + 1], in0=prev, in1=h.t[:, :, c0:c0 + 1],
                                                        op=ALU.subtract), r=[h.b, pb], w=[dx.b])
                    if is_s:
                        OP("pool", lambda e: e.tensor_copy(out=sh_out.t[:, :, s_:s_ + 1], in_=h.t[:, :, c1 - 1:c1]),
                           r=[h.b], w=[sh_out.b])
                if not is_s:
                    OP("pool", lambda e: e.tensor_copy(out=hlast.t[:, :, 0:1], in_=h.t[:, :, W - 1:W]), r=[h.b], w=[hlast.b])
                    if col0 + W == SEQ:
                        OP("pool", lambda e: e.tensor_copy(out=sh_out.t[:, :, 0:1], in_=h.t[:, :, W - 1:W]),
                           r=[h.b], w=[sh_out.b])
                mix(0, xm)
                chk(2)
                for oc in range(8):
                    bank = banks()
                    for dc in range(8):
                        OP("pe", lambda e: e.matmul(bank.t[:, 0:W], lhsT=Wr.t[:, dc, oc * 128:(oc + 1) * 128],
                                                    rhs=xm.t[:, dc, 0:W], start=(dc == 0), stop=(dc == 7)),
                           r=[Wr.b, xm.b], w=[bank.b])
                    OP("act", lambda e: e.activation(out=r_f.t[:, oc, 0:W], in_=bank.t[:, 0:W], func=AF.Copy),
                       r=[bank.b], w=[r_f.b])
                mix(1, xm)
                chk(3)
                bank = proj_small(w1, 64, xm)
                OP("act", lambda e: e.activation(out=t1b.t[:, 0:W], in_=bank.t[0:64, 0:W], func=AF.Tanh), r=[bank.b], w=[t1b.b])
                for oc in range(8):
                    bank = banks()
                    OP("pe", lambda e: e.matmul(bank.t[:, 0:W], lhsT=w2.t[0:64, oc * 128:(oc + 1) * 128], rhs=t1b.t[0:64, 0:W],
                                                start=True, stop=True), r=[w2.b, t1b.b], w=[bank.b])
                    tw = tmpw()
                    OP("act", lambda e: e.activation(out=tw.t[:, 0:W], in_=bank.t[:, 0:W], func=AF.Sigmoid,
                                                     bias=vec.t[:, 6, oc:oc + 1]), r=[bank.b, vec.b], w=[tw.b])
                    OP("act", lambda e: e.activation(out=w_f.t[:, oc, 0:W], in_=tw.t[:, 0:W], func=AF.Exp,
                                                     scale=-0.6065306597126334), r=[tw.b], w=[w_f.b])
                mix(2, xm)
                chk(4)
                for oc in range(8):
                    bank = banks()
                    for dc in range(8):
                        OP("pe", lambda e: e.matmul(bank.t[:, 0:W], lhsT=Wk.t[:, dc, oc * 128:(oc + 1) * 128],
                                                    rhs=xm.t[:, dc, 0:W], start=(dc == 0), stop=(dc == 7)),
                           r=[Wk.b, xm.b], w=[bank.b])
                    OP("act", lambda e: e.activation(out=k_f.t[:, oc, 0:W], in_=bank.t[:, 0:W], func=AF.Copy),
                       r=[bank.b], w=[k_f.b])
                mix(3, xmv)
                if j >= 1:
                    bank = proj_small(v1, 32, xmv)
                    OP("act", lambda e: e.activation(out=t4b.t[:, 0:W], in_=bank.t[0:32, 0:W], func=AF.Copy), r=[bank.b], w=[t4b.b])
                mix(4, xm)
                bank = proj_small(a1, 64, xm)
                OP("act", lambda e: e.activation(out=t1b.t[:, 0:W], in_=bank.t[0:64, 0:W], func=AF.Copy), r=[bank.b], w=[t1b.b])
                for oc in range(8):
                    bank = banks()
                    OP("pe", lambda e: e.matmul(bank.t[:, 0:W], lhsT=a2.t[0:64, oc * 128:(oc + 1) * 128], rhs=t1b.t[0:64, 0:W],
                                                start=True, stop=True), r=[a2.b, t1b.b], w=[bank.b])
                    OP("act", lambda e: e.activation(out=a_f.t[:, oc, 0:W], in_=bank.t[:, 0:W], func=AF.Sigmoid,
                                                     bias=vec.t[:, 7, oc:oc + 1]), r=[bank.b, vec.b], w=[a_f.b])
                mix(5, xm)
                bank = proj_small(g1, 128, xm)
                OP("act", lambda e: e.activation(out=sgA.t[:, 0:W], in_=bank.t[:, 0:W], func=AF.Sigmoid), r=[bank.b], w=[sgA.b])
                bank = proj_small(g1, 32, xm, c0=128)
                OP("act", lambda e: e.activation(out=sgB.t[:, 0:W], in_=bank.t[0:32, 0:W], func=AF.Sigmoid), r=[bank.b], w=[sgB.b])
                chk(5)
                for dc in range(8):
                    OP("act", lambda e: e.activation(out=kk.t[:, dc, 0:W], in_=k_f.t[:, dc, 0:W], func=AF.Identity,
                                                     scale=vec.t[:, 8, dc:dc + 1]), r=[k_f.b, vec.b], w=[kk.b])
                OP("act", lambda e: e.activation(out=sq.t[:, :, 0:W], in_=kk.t[:, :, 0:W], func=AF.Square), r=[kk.b], w=[sq.b])
                for hf in range(2):
                    bank = banks()
                    for d4 in range(4):
                        OP("pe", lambda e: e.matmul(bank.t[:, d4 * W:(d4 + 1) * W], lhsT=blk64_bf.t[:], rhs=sq.t[:, hf * 4 + d4, 0:W],
                                                    start=True, stop=True), r=[blk64_bf.b, sq.b], w=[bank.b])
                    OP("dve", lambda e: e.tensor_scalar(out=rs.t[:, hf * 4:hf * 4 + 4, 0:W],
                                                        in0=bank.t[:, 0:4 * W].rearrange("p (c w) -> p c w", w=W),
                                                        scalar1=1e-24, scalar2=None, op0=ALU.max), r=[bank.b], w=[rs.b])
                OP("act", lambda e: e.activation(out=rs.t[:, :, 0:W], in_=rs.t[:, :, 0:W], func=AF.Sqrt), r=[rs.b], w=[rs.b])
                OP("dve", lambda e: e.reciprocal(out=rs.t[:, :, 0:W], in_=rs.t[:, :, 0:W]), r=[rs.b], w=[rs.b])
                OP("dve", lambda e: e.tensor_tensor(out=kk.t[:, :, 0:W], in0=kk.t[:, :, 0:W], in1=rs.t[:, :, 0:W], op=ALU.mult),
                   r=[kk.b, rs.b], w=[kk.b])
                for dc in range(8):
                    OP("dve", lambda e: e.tensor_scalar(out=kp.t[:, dc, 0:W], in0=a_f.t[:, dc, 0:W], scalar1=vec.t[:, 9, dc:dc + 1],
                                                        scalar2=omk.t[:, dc:dc + 1], op0=ALU.mult, op1=ALU.add),
                       r=[a_f.b, vec.b, omk.b], w=[kp.b])
                OP("dve", lambda e: e.tensor_tensor(out=kp.t[:, :, 0:W], in0=kp.t[:, :, 0:W], in1=k_f.t[:, :, 0:W], op=ALU.mult),
                   r=[kp.b, k_f.b], w=[kp.b])
                OP("dve", lambda e: e.tensor_tensor(out=bb.t[:, :, 0:W], in0=kk.t[:, :, 0:W], in1=a_f.t[:, :, 0:W], op=ALU.mult),
                   r=[kk.b, a_f.b], w=[bb.b])
                for dc in range(8):
                    OP("dve", lambda e: e.scalar_tensor_tensor(out=rkp.t[:, dc, 0:W], in0=r_f.t[:, dc, 0:W],
                                                               scalar=vec.t[:, 10, dc:dc + 1], in1=kp.t[:, dc, 0:W],
                                                               op0=ALU.mult, op1=ALU.mult), r=[r_f.b, vec.b, kp.b], w=[rkp.b])
                chk(6)
                msi = {128: msi_bf, 32: msi32_bf, 8: msi8_bf}[C]
                ml = {128: maskL_bf, 32: ml32_bf, 8: ml8_bf}[C]
                nlev = {128: 7, 32: 5, 8: 3}[C]
                for c in range(nch):
                    cs = c * C
                    Sf = Sfs if is_s else Sfp
                    Sb = Sbs if is_s else Sbp
                    if is_s:
                        DMA([(Sf.t[:], wkvT0[j, c])], r=Sb.bs, w=Sf.bs)
                        for hb in range(2):
                            sl = slice(hb * 64, hb * 64 + 64)
                            OP("act", lambda e: e.activation(out=Sb.t[sl, :, hb * 64:hb * 64 + 64], in_=Sf.t[sl, :, :], func=AF.Copy),
                               r=Sf.bs, w=Sb.bs)
                    for hf in range(2):
                        bank = banks()
                        for dc in range(8):
                            OP("pe", lambda e: e.matmul(bank.t[0:C, 0:512], lhsT=xmv.t[:, dc, cs:cs + C],
                                                        rhs=Wv.t[:, dc, hf * 512:(hf + 1) * 512], start=(dc == 0), stop=(dc == 7)),
                               r=[xmv.b, Wv.b], w=[bank.b])
                        OP("act", lambda e: e.activation(out=vtok.t[0:C, hf * 512:(hf + 1) * 512], in_=bank.t[0:C, 0:512], func=AF.Copy),
                           r=[bank.b], w=[vtok.b])
                    tix = tile_of(col0 + cs)
                    if j == 0:
                        DMA([(vfirst[col0 + cs:col0 + cs + C, :], vtok.t[0:C, :])], r=[vtok.b], w=[vf_b[tix]], sync=vtok.b)
                    else:
                        DMA([(vft.t[0:C, :], vfirst[col0 + cs:col0 + cs + C, :])], r=[vf_b[tix]], w=[vft.b])
                        for hf in range(2):
                            bank = banks()
                            OP("pe", lambda e: e.matmul(bank.t[0:C, 0:512], lhsT=t4b.t[0:32, cs:cs + C],
                                                        rhs=v2.t[0:32, hf * 512:(hf + 1) * 512], start=True, stop=True),
                               r=[t4b.b, v2.b], w=[bank.b])
                            OP("dve", lambda e: e.tensor_tensor(out=gt.t[0:C, hf * 512:(hf + 1) * 512], in0=bank.t[0:C, 0:512],
                                                                in1=v0row.t[0:C, hf * 512:(hf + 1) * 512], op=ALU.add),
                               r=[bank.b, v0row.b], w=[gt.b])
                        OP("act", lambda e: e.activation(out=gt.t[0:C, :], in_=gt.t[0:C, :], func=AF.Sigmoid), r=[gt.b], w=[gt.b])
                        OP("dve", lambda e: e.tensor_tensor(out=vft.t[0:C, :], in0=vft.t[0:C, :], in1=vtok.t[0:C, :], op=ALU.subtract),
                           r=[vft.b, vtok.b], w=[vft.b])
                        OP("dve", lambda e: e.tensor_tensor(out=vft.t[0:C, :], in0=vft.t[0:C, :], in1=gt.t[0:C, :], op=ALU.mult),
                           r=[vft.b, gt.b], w=[vft.b])
                        OP("dve", lambda e: e.tensor_tensor(out=vtok.t[0:C, :], in0=vtok.t[0:C, :], in1=vft.t[0:C, :], op=ALU.add),
                           r=[vft.b, vtok.b], w=[vtok.b])
                    chk(7)
                    OP("pool", lambda e: e.memset(wsh.t[:, :, 0:1], 1.0), w=[wsh.b])
                    OP("pool", lambda e: e.tensor_copy(out=wsh.t[:, :, 1:C], in_=w_f.t[:, :, cs:cs + C - 1]), r=[w_f.b], w=[wsh.b])
                    for dc in range(8):
                        OP("dve", lambda e: e.tensor_tensor_scan(out=Pex.t[:, dc, 0:C], data0=wsh.t[:, dc, 0:C], data1=zeros_f.t[:, 0:C],
                                                                 initial=1.0, op0=ALU.mult, op1=ALU.add),
                           r=[wsh.b, zeros_f.b], w=[Pex.b])
                    OP("dve", lambda e: e.tensor_tensor(out=Pin.t[:, :, 0:C], in0=Pex.t[:, :, 0:C], in1=w_f.t[:, :, cs:cs + C], op=ALU.mult),
                       r=[Pex.b, w_f.b], w=[Pin.b])
                    OP("dve", lambda e: e.scalar_tensor_tensor(out=AR.t[:, :, 0:C], in0=kk.t[:, :, cs:cs + C], scalar=-1.0,
                                                               in1=Pex.t[:, :, 0:C], op0=ALU.mult, op1=ALU.mult),
                       r=[kk.b, Pex.b], w=[AR.b])
                    OP("pool", lambda e: e.tensor_tensor(out=AR.t[:, :, C:2 * C], in0=r_f.t[:, :, cs:cs + C], in1=Pin.t[:, :, 0:C], op=ALU.mult),
                       r=[r_f.b, Pin.b], w=[AR.b])
                    OP("dve", lambda e: e.reciprocal(out=Pex.t[:, :, 0:C], in_=Pin.t[:, :, 0:C]), r=[Pin.b], w=[Pex.b])
                    OP("pool", lambda e: e.tensor_tensor(out=Kt.t[:, :, 0:C], in0=kp.t[:, :, cs:cs + C], in1=Pex.t[:, :, 0:C], op=ALU.mult),
                       r=[kp.b, Pex.b], w=[Kt.b])
                    OP("pool", lambda e: e.tensor_tensor(out=Bt.t[:, :, 0:C], in0=bb.t[:, :, cs:cs + C], in1=Pex.t[:, :, 0:C], op=ALU.mult),
                       r=[bb.b, Pex.b], w=[Bt.b])
                    chk(8)
                    for (srcT, dstT) in ((Bt, BhT), (Kt, KhT)):
                        for half in range(2):
                            for p4 in range(4):
                                p = half * 4 + p4
                                OP("pe", lambda e: e.transpose(psb1.t[0:C, p4 * 128:(p4 + 1) * 128], in_=srcT.t[:, p, 0:C], identity=ident_f.t[:]),
                                   r=[srcT.b, ident_f.b], w=[psb1.b])
                            OP("act", lambda e: e.activation(out=dstT.t[0:C, half * 512:(half + 1) * 512], in_=psb1.t[0:C, 0:512], func=AF.Copy),
                               r=[psb1.b], w=[dstT.b])
                    chk(9)
                    for p in range(8):
                        bA = banks(); bB = banks(); bL = banks()
                        BtZ = [BtZs[hb_]() for hb_ in range(2)]
                        KtZ = [KtZs[hb_]() for hb_ in range(2)]
                        for hb in range(2):
                            OP("pool", lambda e: e.tensor_scalar(out=BtZ[hb].t[:, 0:C], in0=Bt.t[:, p, 0:C], scalar1=misc_f.t[:, hb:hb + 1],
                                                                 scalar2=None, op0=ALU.mult), r=[Bt.b, misc_f.b], w=[BtZ[hb].b])
                            OP("pool", lambda e: e.tensor_scalar(out=KtZ[hb].t[:, 0:C], in0=Kt.t[:, p, 0:C], scalar1=misc_f.t[:, hb:hb + 1],
                                                                 scalar2=None, op0=ALU.mult), r=[Kt.b, misc_f.b], w=[KtZ[hb].b])
                        for hb in range(2):
                            sl = slice(hb * 64, hb * 64 + 64)
                            OP("pe", lambda e: e.matmul(bA.t[0:C, hb * 2 * C:(hb + 1) * 2 * C], lhsT=BtZ[hb].t[:, 0:C], rhs=AR.t[:, p, 0:2 * C],
                                                        start=True, stop=True), r=[BtZ[hb].b, AR.b], w=[bA.b])
                            chk(9.1)
                            OP("pe", lambda e: e.matmul(bB.t[0:C, hb * 2 * C:(hb + 1) * 2 * C], lhsT=KtZ[hb].t[:, 0:C], rhs=AR.t[:, p, 0:2 * C],
                                                        start=True, stop=True), r=[KtZ[hb].b, AR.b], w=[bB.b])
                            chk(9.2)
                            OP("pe", lambda e: e.matmul(bL.t[0:C, hb * C:(hb + 1) * C], lhsT=AR.t[:, p, 0:C], rhs=BtZ[hb].t[:, 0:C],
                                                        start=True, stop=True), r=[BtZ[hb].b, AR.b], w=[bL.b])
                        chk(9.3)
                        mA = mAs(); mB = mBs(); l0 = l0s()
                        OP("dve", lambda e: e.tensor_tensor(out=mA.t[0:C, 0:4 * C], in0=bA.t[0:C, 0:4 * C], in1=msi.t[0:C, 0:4 * C], op=ALU.mult),
                           r=[bA.b, msi.b], w=[mA.b])
                        chk(9.4)
                        OP("dve", lambda e: e.tensor_tensor(out=mB.t[0:C, 0:4 * C], in0=bB.t[0:C, 0:4 * C], in1=msi.t[0:C, 0:4 * C], op=ALU.mult),
                           r=[bB.b, msi.b], w=[mB.b])
                        chk(9.5)
                        l0r = l0rs()
                        OP("act", lambda e: e.activation(out=l0r.t[0:C, 0:2 * C], in_=bL.t[0:C, 0:2 * C], func=AF.Copy), r=[bL.b], w=[l0r.b])
                        OP("pool", lambda e: e.tensor_tensor(out=l0.t[0:C, 0:2 * C], in0=l0r.t[0:C, 0:2 * C], in1=ml.t[0:C, 0:2 * C], op=ALU.mult),
                           r=[l0r.b, ml.b], w=[l0.b])
                        Ls = [l0.t[0:C, hb * C:(hb + 1) * C] for hb in range(2)]
                        LTs = [mA.t[0:C, hb * 2 * C:hb * 2 * C + C] for hb in range(2)]
                        Lb, LTb = l0.b, mA.b
                        chk(10)
                        bX = banks()
                        for hb in range(2):
                            OP("pe", lambda e: e.matmul(bX.t[0:C, hb * 64:hb * 64 + 64], lhsT=AR.t[:, p, 0:C], rhs=Sb.t[:, p, hb * 64:hb * 64 + 64],
                                                        start=True, stop=False), r=[AR.b, Sb.bs[p]], w=[bX.b])
                            chk(10.1 + hb * 0.2)
                            OP("pe", lambda e: e.matmul(bX.t[0:C, hb * 64:hb * 64 + 64], lhsT=mB.t[0:C, hb * 2 * C:hb * 2 * C + C],
                                                        rhs=vb.t[0:C, (2 * p + hb) * 64:(2 * p + hb) * 64 + 64], start=False, stop=True),
                               r=[mB.b, vb.b], w=[bX.b])
                            chk(10.2 + hb * 0.2)
                        Xf = Xfs()
                        OP("dve", lambda e: e.tensor_copy(out=Xf.t[0:C, :], in_=bX.t[0:C, 0:128]), r=[bX.b], w=[Xf.b])
                        Xb = Xf
                        chk(11)
                        for lv_i in range(nlev):
                            bank = banks()
                            for hb in range(2):
                                OP("pe", lambda e: e.matmul(bank.t[0:C, hb * 64:hb * 64 + 64], lhsT=LTs[hb], rhs=Xb.t[0:C, hb * 64:hb * 64 + 64],
                                                            start=True, stop=True), r=[LTb, Xb.b], w=[bank.b])
                            OP("dve", lambda e: e.tensor_tensor(out=Xf.t[0:C, :], in0=bank.t[0:C, 0:128], in1=Xf.t[0:C, :], op=ALU.add),
                               r=[bank.b, Xf.b], w=[Xf.b])
                            if lv_i < nlev - 1:
                                b2 = banks()
                                for hb in range(2):
                                    OP("pe", lambda e: e.matmul(b2.t[0:C, (2 * hb) * C:(2 * hb + 1) * C], lhsT=LTs[hb], rhs=Ls[hb],
                                                                start=True, stop=True), r=[Lb, LTb], w=[b2.b])
                                    OP("pe", lambda e: e.matmul(b2.t[0:C, (2 * hb + 1) * C:(2 * hb + 2) * C], lhsT=Ls[hb], rhs=LTs[hb],
                                                                start=True, stop=True), r=[Lb, LTb], w=[b2.b])
                                lv = lvs()
                                OP("act", lambda e: e.activation(out=lv.t[0:C, 0:4 * C], in_=b2.t[0:C, 0:4 * C], func=AF.Copy), r=[b2.b], w=[lv.b])
                                Ls = [lv.t[0:C, (2 * hb) * C:(2 * hb + 1) * C] for hb in range(2)]
                                LTs = [lv.t[0:C, (2 * hb + 1) * C:(2 * hb + 2) * C] for hb in range(2)]
                                Lb = LTb = lv.b
                        Ub = Xb
                        chk(12)
                        ybk = yb[p // 4]
                        for hb in range(2):
                            oc0 = (p % 4) * 128 + hb * 64
                            OP("pe", lambda e: e.matmul(ybk.t[0:C, oc0:oc0 + 64], lhsT=AR.t[:, p, C:2 * C], rhs=Sb.t[:, p, hb * 64:hb * 64 + 64],
                                                        start=True, stop=False), r=[AR.b, Sb.bs[p]], w=[ybk.b])
                            OP("pe", lambda e: e.matmul(ybk.t[0:C, oc0:oc0 + 64], lhsT=mA.t[0:C, hb * 2 * C + C:hb * 2 * C + 2 * C],
                                                        rhs=Ub.t[0:C, hb * 64:hb * 64 + 64], start=False, stop=False), r=[mA.b, Ub.b], w=[ybk.b])
                            OP("pe", lambda e: e.matmul(ybk.t[0:C, oc0:oc0 + 64], lhsT=mB.t[0:C, hb * 2 * C + C:hb * 2 * C + 2 * C],
                                                        rhs=vb.t[0:C, (2 * p + hb) * 64:(2 * p + hb) * 64 + 64], start=False, stop=True),
                               r=[mB.b, vb.b], w=[ybk.b])
                        chk(13)
                        bS = banks()
                        OP("pe", lambda e: e.matmul(bS.t[:, 0:128], lhsT=BhT.t[0:C, p * 128:(p + 1) * 128], rhs=Ub.t[0:C, 0:128],
                                                    start=True, stop=False), r=[BhT.b, Ub.b], w=[bS.b])
                        OP("pe", lambda e: e.matmul(bS.t[:, 0:128], lhsT=KhT.t[0:C, p * 128:(p + 1) * 128], rhs=vb.t[0:C, p * 128:(p + 1) * 128],
                                                    start=False, stop=True), r=[KhT.b, vb.b], w=[bS.b])
                        for hb in range(2):
                            sl = slice(hb * 64, hb * 64 + 64)
                            OP("dve", lambda e: e.tensor_tensor(out=Sf.t[sl, p, :], in0=Sf.t[sl, p, :], in1=bS.t[sl, hb * 64:hb * 64 + 64], op=ALU.add),
                               r=[Sf.bs[p], bS.b], w=[Sf.bs[p]])
                            OP("dve", lambda e: e.tensor_scalar(out=Sf.t[sl, p, :], in0=Sf.t[sl, p, :], scalar1=Pin.t[sl, p, C - 1:C], scalar2=None,
                                                                op0=ALU.mult), r=[Sf.bs[p], Pin.b], w=[Sf.bs[p]])
                            OP("act", lambda e: e.activation(out=Sb.t[sl, p, hb * 64:hb * 64 + 64], in_=Sf.t[sl, p, :], func=AF.Copy),
                               r=[Sf.bs[p]], w=[Sb.bs[p]])
                    chk(14)
                    for hf in range(2):
                        OP("act", lambda e: e.activation(out=yf.t[0:C, hf * 512:(hf + 1) * 512], in_=yb[hf].t[0:C, 0:512], func=AF.Copy),
                           r=[yb[hf].b], w=[yf.b])
                    yf3 = yf.t[0:C, :].rearrange("p (h x) -> p h x", x=64)
                    bc = lambda t_: t_.t[0:C, :].rearrange("p (h o) -> p h o", o=1).to_broadcast([C, 16, 64])
                    OP("dve", lambda e: e.tensor_reduce(out=s1.t[0:C, :], in_=yf3, axis=AX.X, op=ALU.add), r=[yf.b], w=[s1.b])
                    OP("act", lambda e: e.activation(out=tmpv.t[0:C, :], in_=yf.t[0:C, :], func=AF.Square), r=[yf.b], w=[tmpv.b])
                    OP("dve", lambda e: e.tensor_reduce(out=s2.t[0:C, :], in_=tmpv.t[0:C, :].rearrange("p (h x) -> p h x", x=64),
                                                        axis=AX.X, op=ALU.add), r=[tmpv.b], w=[s2.b])
                    OP("dve", lambda e: e.tensor_scalar(out=mean.t[0:C, :], in0=s1.t[0:C, :], scalar1=1.0 / 64, scalar2=None, op0=ALU.mult),
                       r=[s1.b], w=[mean.b])
                    OP("dve", lambda e: e.tensor_tensor(out=msq.t[0:C, :], in0=mean.t[0:C, :], in1=mean.t[0:C, :], op=ALU.mult),
                       r=[mean.b], w=[msq.b])
                    OP("dve", lambda e: e.scalar_tensor_tensor(out=msq.t[0:C, :], in0=s2.t[0:C, :], scalar=1.0 / 64, in1=msq.t[0:C, :],
                                                               op0=ALU.mult, op1=ALU.subtract), r=[s2.b, msq.b], w=[msq.b])
                    OP("act", lambda e: e.activation(out=rstdg.t[0:C, :], in_=msq.t[0:C, :], func=AF.Sqrt, bias=eps_g.t[0:C, 0:1]),
                       r=[msq.b, eps_g.b], w=[rstdg.b])
                    OP("dve", lambda e: e.reciprocal(out=rstdg.t[0:C, :], in_=rstdg.t[0:C, :]), r=[rstdg.b], w=[rstdg.b])
                    OP("dve", lambda e: e.tensor_tensor(out=yf3, in0=yf3, in1=bc(mean), op=ALU.subtract), r=[yf.b, mean.b], w=[yf.b])
                    OP("dve", lambda e: e.tensor_tensor(out=yf3, in0=yf3, in1=bc(rstdg), op=ALU.mult), r=[yf.b, rstdg.b], w=[yf.b])
                    OP("dve", lambda e: e.tensor_tensor(out=yf.t[0:C, :], in0=yf.t[0:C, :], in1=lnw.t[0:C, :], op=ALU.mult), r=[yf.b, lnw.b], w=[yf.b])
                    OP("pool", lambda e: e.tensor_tensor(out=yf.t[0:C, :], in0=yf.t[0:C, :], in1=lnb.t[0:C, :], op=ALU.add), r=[yf.b, lnb.b], w=[yf.b])
                    chk(15)
                    bR = banks()
                    for p in range(8):
                        OP("pe", lambda e: e.matmul(bR.t[0:C, 2 * p:2 * p + 2], lhsT=rkp.t[:, p, cs:cs + C], rhs=blk2_bf.t[:, 0:2],
                                                    start=True, stop=True), r=[rkp.b, blk2_bf.b], w=[bR.b])
                    OP("act", lambda e: e.activation(out=rk.t[0:C, :], in_=bR.t[0:C, 0:16], func=AF.Copy), r=[bR.b], w=[rk.b])
                    OP("pool", lambda e: e.tensor_tensor(out=tmpv.t[0:C, :].rearrange("p (h x) -> p h x", x=64),
                                                         in0=vtok.t[0:C, :].rearrange("p (h x) -> p h x", x=64), in1=bc(rk), op=ALU.mult),
                       r=[vtok.b, rk.b], w=[tmpv.b])
                    OP("pool", lambda e: e.tensor_tensor(out=yf.t[0:C, :], in0=yf.t[0:C, :], in1=tmpv.t[0:C, :], op=ALU.add),
                       r=[yf.b, tmpv.b], w=[yf.b])
                    for hf in range(2):
                        bG = banks()
                        OP("pe", lambda e: e.matmul(bG.t[0:C, 0:512], lhsT=sgA.t[:, cs:cs + C], rhs=g2a.t[:, hf * 512:(hf + 1) * 512],
                                                    start=True, stop=False), r=[sgA.b, g2a.b], w=[bG.b])
                        OP("pe", lambda e: e.matmul(bG.t[0:C, 0:512], lhsT=sgB.t[0:32, cs:cs + C], rhs=g2b.t[0:32, hf * 512:(hf + 1) * 512],
                                                    start=False, stop=True), r=[sgB.b, g2b.b], w=[bG.b])
                        OP("dve", lambda e: e.tensor_tensor(out=yg.t[0:C, hf * 512:(hf + 1) * 512], in0=yf.t[0:C, hf * 512:(hf + 1) * 512],
                                                            in1=bG.t[0:C, 0:512], op=ALU.mult), r=[yf.b, bG.b], w=[yg.b])
                    chk(16)
                    for half in range(2):
                        for d4 in range(4):
                            dc = half * 4 + d4
                            OP("pe", lambda e: e.transpose(psb3.t[:, d4 * C:(d4 + 1) * C], in_=yg.t[0:C, dc * 128:(dc + 1) * 128],
                                                           identity=ident_bf.t[0:C, 0:C]), r=[yg.b, ident_bf.b], w=[psb3.b])
                        OP("act", lambda e: e.activation(out=ygT.t[:, half * 4:half * 4 + 4, cs:cs + C],
                                                         in_=psb3.t[:, 0:4 * C].rearrange("p (c w) -> p c w", w=C),
                                                         func=AF.Copy), r=[psb3.b], w=[ygT.b])
                    if is_s:
                        DMA([(wkv_o[j, 1 + c], Sf.t[:])], r=Sf.bs, w=[out_b], sync=Sf.bs[0])
                DMA([(x_view(ms, col0, W), ygT.t[:, :, 0:W])], r=[ygT.b], w=[ms_b[ti]], sync=ygT.b)
                if col0 + W == SEQ:
                    DMA([(wkv_o[j, 0], Sfp.t[:])], r=Sfp.bs, w=[out_b], sync=Sfp.bs[0])
            P.disabled = False
            DMA([(shift_o[j], sh_out.t[:])], r=[sh_out.b], w=[out_b], sync=sh_out.b)
            P.flush()

    def dsa_sample(i, KTZ, KITZ, Vcur):
        j = i // 2
        QSCALE = 0.125
        c_ik = cache_ik[j]
        c_k = cache_k[j]
        c_v = cache_v[j]
        with contextlib.ExitStack() as st:
            banks = RR([kb.ps(st, "sps%d" % k, [128, 512], F32) for k in range(5)])
            psO = kb.ps(st, "spsO", [128, 512], F32)
            psb = RR([kb.ps(st, "spsb%d" % k, [128, 512], BF16) for k in range(2)])
            qS = kb.sb(st, "qS", [128, 8, 32], BF16)
            qiS = kb.sb(st, "qiS", [128, 4, 32], BF16)
            DMA([(qS.t[:], x_view(qs, SEQ, 32))], r=[qs_b[8]], w=[qS.b])
            DMA([(qiS.t[:], qis.rearrange("(c p) t -> p c t", p=128)[:, :, SEQ:SEQ + 32])], r=[qis_b[8]], w=[qiS.b])
            ptt = kb.sb(st, "ptt", [64, NSMP], I32)
            DMA([(ptt.t[:], ptT)], w=[ptt.b])
            idx_ki = [kb.sb(st, "idx_ki%d" % k, [64, 1], I32) for k in range(NSMP * 4)]
            idx_kv = [kb.sb(st, "idx_kv%d" % k, [64, 1], I32) for k in range(NSMP * 16)]
            for s_ in range(NSMP):
                for c4 in range(4):
                    k_ = s_ * 4 + c4
                    OP("dve", lambda e: e.tensor_scalar(out=idx_ki[k_].t[:, 0:1], in0=ptt.t[:, s_:s_ + 1], scalar1=4, scalar2=c4,
                                                        op0=ALU.mult, op1=ALU.add), r=[ptt.b], w=[idx_ki[k_].b])
                for c16 in range(16):
                    k_ = s_ * 16 + c16
                    OP("dve", lambda e: e.tensor_scalar(out=idx_kv[k_].t[:, 0:1], in0=ptt.t[:, s_:s_ + 1], scalar1=16, scalar2=c16,
                                                        op0=ALU.mult, op1=ALU.add), r=[ptt.b], w=[idx_kv[k_].b])
            QiA = kb.sb(st, "QiA", [64, 8, 32], BF16)
            ps = banks()
            for h_ in range(8):
                c, hb = h_ // 2, h_ % 2
                OP("pe", lambda e: e.matmul(ps.t[0:64, h_ * 32:(h_ + 1) * 32], lhsT=sel_bf[hb].t[:, 0:64], rhs=qiS.t[:, c, :], start=True, stop=True),
                   r=[sel_bf[hb].b, qiS.b], w=[ps.b])
            OP("act", lambda e: e.activation(out=QiA.t[:, :, :], in_=ps.t[0:64, 0:256].rearrange("p (h t) -> p h t", t=32), func=AF.Copy),
               r=[ps.b], w=[QiA.b])
            QiSq = [kb.sb(st, "QiSq%d" % s_, [64, 8, 8], BF16) for s_ in range(NSMP)]
            for s_ in range(NSMP):
                OP("pool", lambda e: e.tensor_copy(out=QiSq[s_].t[:, :, :], in_=QiA.t[:, :, TS * s_:TS * s_ + TS]), r=[QiA.b], w=[QiSq[s_].b])
            wcol = kb.sb(st, "wcol", [64, NSMP], F32)
            DMA([(wcol.t[:, s_:s_ + 1], wsc[s_]) for s_ in range(NSMP)], r=[wsc_b], w=[wcol.b])
            Wsel = kb.sb(st, "Wsel", [64, NSMP, 32], F32)
            OP("pool", lambda e: e.memset(Wsel.t[:], 0.0), w=[Wsel.b])
            for s_ in range(NSMP):
                OP("pool", lambda e: e.tensor_scalar(out=Wsel.t[:, s_, TS * s_:TS * s_ + TS], in0=misc_f.t[0:64, 2:10], scalar1=wcol.t[:, s_:s_ + 1],
                                                     scalar2=None, op0=ALU.mult), r=[misc_f.b, wcol.b], w=[Wsel.b])
            NK = 8192
            mT = kb.sb(st, "mT", [64, 128, 32], BF16)
            mTc = kb.sb(st, "mTc", [8, 32], BF16)
            st_a = contextlib.ExitStack()
            SC = kb.sb(st_a, "SC", [32, NK + 8], F32)
            m01 = kb.sb(st_a, "sm01", [32, NK + 8], BF16)
            m8 = kb.sb(st_a, "sm8", [32, 256], F32)
            with contextlib.ExitStack() as st2:
                kiT = kb.sb(st2, "kiT", [64, NSMP, 32, 64], BF16, nsub=NSMP)
                kigs = RR([kb.sb(st2, "kig%d" % k, [64, 2048], F32) for k in range(1)])
                kgb = kb.sb(st2, "kgb", [64, 32, 64], BF16)
                rss = [kb.sb(st2, "rs%d" % s_, [64, 512], F32) for s_ in range(NSMP)]
                rcs = [kb.sb(st2, "rc%d" % s_, [64, 8], F32) for s_ in range(NSMP)]
                for c4 in range(4):
                    for s_ in range(NSMP):
                        kig = kigs()
                        k_ = s_ * 4 + c4
                        DMA(None, r=[idx_ki[k_].b], w=[kig.b], queue="pool",
                            fn=lambda e: e.indirect_dma_start(out=kig.t[:], out_offset=None, in_=c_ik,
                                                              in_offset=bass.IndirectOffsetOnAxis(ap=idx_ki[k_].t[:, 0:1], axis=0)))
                        OP("act", lambda e: e.activation(out=kgb.t[:, :, :], in_=kig.t[:, :].rearrange("p (r d) -> p r d", d=64), func=AF.Copy),
                           r=[kig.b], w=[kgb.b])
                        for g in range(4):
                            pb = psb()
                            for r8 in range(8):
                                r_ = g * 8 + r8
                                OP("pe", lambda e: e.transpose(pb.t[0:64, r8 * 64:(r8 + 1) * 64], in_=kgb.t[:, r_, :], identity=ident_bf.t[0:64, 0:64]),
                                   r=[kgb.b, ident_bf.b], w=[pb.b])
                            OP("act", lambda e: e.activation(out=kiT.t[:, s_, g * 8:(g + 1) * 8, :], in_=pb.t[0:64, 0:512].rearrange("p (r g) -> p r g", g=64),
                                                             func=AF.Copy), r=[pb.b], w=[kiT.bs[s_]])
                    for kt in range(4):
                        for s_ in range(NSMP):
                            ps = banks()
                            OP("pe", lambda e: e.matmul(ps.t[0:64, 0:512], lhsT=QiSq[s_].t[:, :, :], rhs=kiT.t[:, s_, kt * 8:(kt + 1) * 8, :],
                                                        start=True, stop=True), r=[QiSq[s_].b, kiT.bs[s_]], w=[ps.b])
                            OP("act", lambda e: e.activation(out=rss[s_].t[:, :], in_=ps.t[0:64, 0:512], func=AF.Relu), r=[ps.b], w=[rss[s_].b])
                        ps = banks()
                        for s_ in range(NSMP):
                            OP("pe", lambda e: e.matmul(ps.t[0:32, 0:512], lhsT=Wsel.t[:, s_, :], rhs=rss[s_].t[:, :], start=(s_ == 0), stop=(s_ == NSMP - 1)),
                               r=[Wsel.b, rss[s_].b], w=[ps.b])
                        cb = (c4 * 32 + kt * 8) * 64
                        OP("dve", lambda e: e.tensor_copy(out=SC.t[:, cb:cb + 512], in_=ps.t[0:32, 0:512]), r=[ps.b], w=[SC.b])
                for s_ in range(NSMP):
                    ps = banks()
                    OP("pe", lambda e: e.matmul(ps.t[0:64, 0:8], lhsT=QiSq[s_].t[:, :, :], rhs=KITZ.t[0:64, 0, SEQ + TS * s_:SEQ + TS * s_ + TS],
                                                start=True, stop=True), r=[QiSq[s_].b, KITZ.b], w=[ps.b])
                    OP("act", lambda e: e.activation(out=rcs[s_].t[:, :], in_=ps.t[0:64, 0:8], func=AF.Relu), r=[ps.b], w=[rcs[s_].b])
                ps = banks()
                for s_ in range(NSMP):
                    OP("pe", lambda e: e.matmul(ps.t[0:32, 0:8], lhsT=Wsel.t[:, s_, :], rhs=rcs[s_].t[:, :], start=(s_ == 0), stop=(s_ == NSMP - 1)),
                       r=[Wsel.b, rcs[s_].b], w=[ps.b])
                OP("dve", lambda e: e.tensor_tensor(out=SC.t[:, NK:NK + 8], in0=ps.t[0:32, 0:8], in1=misc_f.t[0:32, 10:18], op=ALU.add),
                   r=[ps.b, misc_f.b], w=[SC.b])
                P.flush()
            scw = kb.sb(st_a, "sscw", [32, NK + 8], F32)
            L = NK + 8
            for r_ in range(32):
                srcw = SC if r_ == 0 else scw
                OP("dve", lambda e: e.max(out=m8.t[:, 8 * r_:8 * r_ + 8], in_=srcw.t[:, 0:L]), r=[srcw.b], w=[m8.b])
                if r_ < 31:
                    OP("dve", lambda e: e.match_replace(out=scw.t[:, 0:L], in_to_replace=m8.t[:, 8 * r_:8 * r_ + 8], in_values=srcw.t[:, 0:L],
                                                        imm_value=NEG), r=[srcw.b, m8.b], w=[scw.b])
            OP("dve", lambda e: e.tensor_scalar(out=m01.t[:, 0:L], in0=SC.t[:, 0:L], scalar1=m8.t[:, 255:256], scalar2=None, op0=ALU.is_ge),
               r=[SC.b, m8.b], w=[m01.b])
            for g in range(16):
                pb = psb()
                for r8 in range(8):
                    r_ = g * 8 + r8
                    OP("pe", lambda e: e.transpose(pb.t[0:64, r8 * 32:(r8 + 1) * 32], in_=m01.t[0:32, r_ * 64:(r_ + 1) * 64], identity=ident_bf.t[0:32, 0:32]),
                       r=[m01.b, ident_bf.b], w=[pb.b])
                OP("act", lambda e: e.activation(out=mT.t[:, g * 8:(g + 1) * 8, :], in_=pb.t[0:64, 0:256].rearrange("p (r t) -> p r t", t=32), func=AF.Copy),
                   r=[pb.b], w=[mT.b])
            pb = psb()
            OP("pe", lambda e: e.transpose(pb.t[0:8, 0:32], in_=m01.t[0:32, NK:NK + 8], identity=ident_bf.t[0:32, 0:32]), r=[m01.b, ident_bf.b], w=[pb.b])
            OP("act", lambda e: e.activation(out=mTc.t[:, :], in_=pb.t[0:8, 0:32], func=AF.Copy), r=[pb.b], w=[mTc.b])
            P.flush()
            st_a.close()
            oTs = kb.sb(st, "oTs", [128, 8, 32], BF16)
            qZ = kb.sb(st, "qZ", [128, 4, 32], BF16)
            acc = kb.sb(st, "acc", [32, 260], F32)
            Kgs = RR([kb.sb(st, "Kg%d" % k, [64, 2048], F32) for k in range(1)])
            Vgs = RR([kb.sb(st, "Vg%d" % k, [64, 2048], F32) for k in range(1)])
            Kgb = kb.sb(st, "Kgb", [64, 8, 4, 64], BF16)
            Vbs = RR([kb.sb(st, "Vb%d" % k, [64, 8, 4, 65], BF16) for k in range(2)])
            for vb_ in Vbs.tiles:
                OP("pool", lambda e: e.memset(vb_.t[:], 1.0), w=[vb_.b])
            KTg = kb.sb(st, "KTg", [128, 8, 2, 64], BF16)
            ets = RR([kb.sb(st, "set%d" % k, [64, 512], BF16) for k in range(2)])
            pTs = [kb.sb(st, "spT%d" % k, [64, 512], BF16) for k in range(2)]
            ec = kb.sb(st, "ec", [8, 128], BF16)
            pc = kb.sb(st, "pc", [8, 128], BF16)
            rec = kb.sb(st, "srec", [32, 4, 1], F32)
            osb = kb.sb(st, "osb", [32, 256], BF16)
            for s_ in range(NSMP):
                for kvh in range(4):
                    kp, hb = kvh // 2, kvh % 2
                    OP("pool", lambda e: e.tensor_scalar(out=qZ.t[:, kvh, :].rearrange("p (i t) -> p i t", t=TS),
                                                         in0=qS.t[:, 4 * kp:4 * kp + 4, TS * s_:TS * s_ + TS], scalar1=misc_f.t[:, hb:hb + 1],
                                                         scalar2=None, op0=ALU.mult), r=[qS.b, misc_f.b], w=[qZ.b])
                for c16 in range(16):
                    Kg = Kgs(); Vg = Vgs(); Vb = Vbs()
                    k_ = s_ * 16 + c16
                    DMA(None, r=[idx_kv[k_].b], w=[Kg.b], queue="pool",
                        fn=lambda e: e.indirect_dma_start(out=Kg.t[:], out_offset=None, in_=c_k,
                                                          in_offset=bass.IndirectOffsetOnAxis(ap=idx_kv[k_].t[:, 0:1], axis=0)))
                    DMA(None, r=[idx_kv[k_].b], w=[Vg.b], queue="pool",
                        fn=lambda e: e.indirect_dma_start(out=Vg.t[:], out_offset=None, in_=c_v,
                                                          in_offset=bass.IndirectOffsetOnAxis(ap=idx_kv[k_].t[:, 0:1], axis=0)))
                    OP("act", lambda e: e.activation(out=Kgb.t[:, :, :, :], in_=Kg.t[:, :].rearrange("p (r k d) -> p r k d", k=4, d=64), func=AF.Copy),
                       r=[Kg.b], w=[Kgb.b])
                    OP("pool", lambda e: e.tensor_copy(out=Vb.t[:, :, :, 0:64], in_=Vg.t[:, :].rearrange("p (r k d) -> p r k d", k=4, d=64)),
                       r=[Vg.b], w=[Vb.b])
                    for half in range(2):
                        pb = psb()
                        for q4 in range(4):
                            rl = half * 4 + q4
                            for kp in range(2):
                                sl_ = (q4 * 2 + kp) * 64
                                OP("pe", lambda e: e.transpose(pb.t[:, sl_:sl_ + 64], in_=Kgb.t[:, rl, 2 * kp:2 * kp + 2, :], identity=ident_bf.t[0:64, 0:64]),
                                   r=[Kgb.b, ident_bf.b], w=[pb.b])
                        OP("act", lambda e: e.activation(out=KTg.t[:, half * 4:half * 4 + 4, :, :],
                                                         in_=pb.t[:, 0:512].rearrange("p (r k g) -> p r k g", k=2, g=64), func=AF.Copy),
                           r=[pb.b], w=[KTg.b])
                    for half in range(2):
                        ps = banks()
                        for q4 in range(4):
                            rl = half * 4 + q4
                            for kvh in range(4):
                                kp = kvh // 2
                                cc = (q4 * 4 + kvh) * 32
                                OP("pe", lambda e: e.matmul(ps.t[0:64, cc:cc + 32], lhsT=KTg.t[:, rl, kp, :], rhs=qZ.t[:, kvh, :], start=True, stop=True),
                                   r=[KTg.b, qZ.b], w=[ps.b])
                        et = ets()
                        OP("act", lambda e: e.activation(out=et.t[:, :], in_=ps.t[0:64, 0:512], func=AF.Exp, scale=QSCALE), r=[ps.b], w=[et.b])
                        r0 = c16 * 8 + half * 4
                        OP("pool", lambda e: e.tensor_tensor(out=pTs[half].t[:, :].rearrange("p (r k t) -> p r k t", k=16, t=TS),
                                                             in0=et.t[:, :].rearrange("p (r k t) -> p r k t", k=16, t=TS),
                                                             in1=mT.t[:, r0:r0 + 4, TS * s_:TS * s_ + TS].rearrange("p r (o t) -> p r o t", o=1).to_broadcast([64, 4, 16, TS]),
                                                             op=ALU.mult), r=[et.b, mT.b], w=[pTs[half].b])
                    for kvh in range(4):
                        for rl in range(8):
                            half, q4 = rl // 4, rl % 4
                            cc = (q4 * 4 + kvh) * 32
                            OP("pe", lambda e: e.matmul(psO.t[0:32, kvh * 65:(kvh + 1) * 65], lhsT=pTs[half].t[:, cc:cc + 32], rhs=Vb.t[:, rl, kvh, :],
                                                        start=(rl == 0), stop=(rl == 7)), r=[pTs[half].b, Vb.b], w=[psO.b])
                    if c16 == 0:
                        OP("dve", lambda e: e.tensor_copy(out=acc.t[:, :], in_=psO.t[0:32, 0:260]), r=[psO.b], w=[acc.b])
                    else:
                        OP("dve", lambda e: e.tensor_tensor(out=acc.t[:, :], in0=acc.t[:, :], in1=psO.t[0:32, 0:260], op=ALU.add),
                           r=[psO.b, acc.b], w=[acc.b])
                ps = banks()
                for kvh in range(4):
                    OP("pe", lambda e: e.matmul(ps.t[0:8, kvh * 32:(kvh + 1) * 32], lhsT=KTZ.t[:, kvh, SEQ + TS * s_:SEQ + TS * s_ + TS], rhs=qZ.t[:, kvh, :],
                                                start=True, stop=True), r=[KTZ.b, qZ.b], w=[ps.b])
                OP("act", lambda e: e.activation(out=ec.t[:, :], in_=ps.t[0:8, 0:128], func=AF.Exp, scale=QSCALE), r=[ps.b], w=[ec.b])
                OP("pool", lambda e: e.tensor_tensor(out=pc.t[:, :].rearrange("p (k t) -> p k t", t=TS), in0=ec.t[:, :].rearrange("p (k t) -> p k t", t=TS),
                                                     in1=mTc.t[:, TS * s_:TS * s_ + TS].rearrange("p (o t) -> p o t", o=1).to_broadcast([8, 16, TS]),
                                                     op=ALU.mult), r=[ec.b, mTc.b], w=[pc.b])
                for kvh in range(4):
                    OP("pe", lambda e: e.matmul(psO.t[0:32, kvh * 65:(kvh + 1) * 65], lhsT=pc.t[:, kvh * 32:(kvh + 1) * 32], rhs=Vcur[s_].t[:, kvh, :],
                                                start=True, stop=True), r=[pc.b, Vcur[s_].b], w=[psO.b])
                OP("dve", lambda e: e.tensor_tensor(out=acc.t[:, :], in0=acc.t[:, :], in1=psO.t[0:32, 0:260], op=ALU.add), r=[psO.b, acc.b], w=[acc.b])
                acc3 = acc.t[:, :].rearrange("p (k d) -> p k d", d=65)
                OP("dve", lambda e: e.reciprocal(out=rec.t[:], in_=acc3[:, :, 64:65]), r=[acc.b], w=[rec.b])
                OP("dve", lambda e: e.tensor_tensor(out=osb.t[:, :].rearrange("p (k d) -> p k d", d=64), in0=acc3[:, :, 0:64],
                                                    in1=rec.t[:].to_broadcast([32, 4, 64]), op=ALU.mult), r=[acc.b, rec.b], w=[osb.b])
                pb = psb()
                for kp in range(2):
                    OP("pe", lambda e: e.transpose(pb.t[:, kp * 32:(kp + 1) * 32], in_=osb.t[0:32, kp * 128:(kp + 1) * 128], identity=ident_bf.t[0:32, 0:32]),
                       r=[osb.b, ident_bf.b], w=[pb.b])
                for kp in range(2):
                    OP("act", lambda e: e.activation(out=oTs.t[:, 4 * kp:4 * kp + 4, TS * s_:TS * s_ + TS],
                                                     in_=pb.t[:, kp * 32:(kp + 1) * 32].rearrange("p (i t) -> p i t", t=TS), func=AF.Copy),
                       r=[pb.b], w=[oTs.b])
            DMA([(x_view(ms, SEQ, 32), oTs.t[:])], r=[oTs.b], w=[ms_b[8]], sync=oTs.b)
            P.flush()

    def dsa_stage(i):
        import os as _os
        j = i // 2
        QSCALE = 0.125
        WISCALE = (8 ** -0.5) * 0.125
        with contextlib.ExitStack() as st0:
            KTZ = kb.sb(st0, "KTZ", [128, 4, NTOK], BF16)
            KITZ = kb.sb(st0, "KITZ", [128, 2, NTOK], BF16)
            Vx = kb.sb(st0, "Vx", [128, 32, 4, 65], BF16)
            Vcur = [kb.sb(st0, "Vcur%d" % s_, [8, 4, 65], BF16) for s_ in range(NSMP)]
            WI = kb.sb(st0, "WI", [128, 32, 8], F32)
            OP("pool", lambda e: e.memset(Vx.t[:], 1.0), w=[Vx.b])
            for s_ in range(NSMP):
                OP("pool", lambda e: e.memset(Vcur[s_].t[:], 1.0), w=[Vcur[s_].b])
            with contextlib.ExitStack() as st:
                win = kb.sb(st, "win", [128, 8, PIN2], BF16)
                wv = att_w_in[j].rearrange("(dc p) f -> p dc f", p=128)
                DMA([(win.t[:, :, 0:1024], wv[:, :, 0:1024])], w=[win.b], queue="pool")
                DMA([(win.t[:, :, 1024:PIN2], wv[:, :, 1024:PIN2])], w=[win.b], queue="pool")
                gains = kb.sb(st, "gains", [128, 3], F32)
                DMA([(gains.t[:], dsavec[j])], w=[gains.b])
                cosT = kb.sb(st, "cosT", [128, NTOK], F32)
                sinT = kb.sb(st, "sinT", [128, NTOK], F32)
                DMA([(cosT.t[:], ropecs[0])], w=[cosT.b])
                DMA([(sinT.t[:], ropecs[1])], w=[sinT.b])
                xt = kb.sb(st, "dxt", [128, 8, 512], F32)
                h = kb.sb(st, "dh", [128, 8, 512], BF16)
                sq = kb.sb(st, "dsq", [128, 8, 512], BF16)
                rst = kb.sb(st, "drst", [128, 512], F32)
                xns = RR([kb.sb(st, "dxn%d" % k, [128, 512], F32) for k in range(2)])
                qst = kb.sb(st, "qst", [128, 8, 512], BF16)
                qist = kb.sb(st, "qist", [128, 4, 512], BF16)
                sqt = kb.sb(st, "sqt", [128, 512], BF16)
                rt = kb.sb(st, "rt", [128, 512], F32)
                qn = kb.sb(st, "qn", [128, 512], F32)
                qnb = kb.sb(st, "qnb", [128, 512], BF16)
                t1 = kb.sb(st, "t1", [128, 512], F32)
                t2 = kb.sb(st, "t2", [128, 512], F32)
                resf = kb.sb(st, "resf", [128, 512], F32)
                vout = kb.sb(st, "vout", [128, 256], F32)
                wiT = kb.sb(st, "wiT", [8, 32], F32)
                banks = RR([kb.ps(st, "dps%d" % k, [128, 512], F32) for k in range(6)])
                for (col0, W) in TILES512:
                    ti = tile_of(col0)
                    is_s = col0 >= SEQ
                    DMA([(xt.t[:, :, 0:W], x_view(xs, col0, W))], r=[xs_b[ti]], w=[xt.b])
                    emit_norm((sq, rst, xns), xt, W, segs_of(col0, W), i, "a", h, banks)

                    def fm_chunk(c0, norm_gain):
                        ps = banks()
                        for dc in range(8):
                            OP("pe", lambda e: e.matmul(ps.t[:, 0:W], lhsT=win.t[:, dc, c0:c0 + 128], rhs=h.t[:, dc, 0:W],
                                                        start=(dc == 0), stop=(dc == 7)), r=[win.b, h.b], w=[ps.b])
                        if norm_gain is not None:
                            OP("act", lambda e: e.activation(out=sqt.t[:, 0:W], in_=ps.t[:, 0:W], func=AF.Square), r=[ps.b], w=[sqt.b])
                            ps2 = banks()
                            OP("pe", lambda e: e.matmul(ps2.t[:, 0:W], lhsT=blk64_bf.t[:], rhs=sqt.t[:, 0:W], start=True, stop=True),
                               r=[blk64_bf.b, sqt.b], w=[ps2.b])
                            OP("act", lambda e: e.activation(out=rt.t[:, 0:W], in_=ps2.t[:, 0:W], func=AF.Sqrt, scale=1.0 / 64,
                                                             bias=eps_n.t[:, 0:1]), r=[ps2.b, eps_n.b], w=[rt.b])
                            OP("dve", lambda e: e.reciprocal(out=rt.t[:, 0:W], in_=rt.t[:, 0:W]), r=[rt.b], w=[rt.b])
                            OP("dve", lambda e: e.scalar_tensor_tensor(out=qn.t[:, 0:W], in0=ps.t[:, 0:W], scalar=gains.t[:, norm_gain:norm_gain + 1],
                                                                       in1=rt.t[:, 0:W], op0=ALU.mult, op1=ALU.mult),
                               r=[ps.b, gains.b, rt.b], w=[qn.b])
                        else:
                            OP("act", lambda e: e.activation(out=qn.t[:, 0:W], in_=ps.t[:, 0:W], func=AF.Copy), r=[ps.b], w=[qn.b])
                        OP("pool", lambda e: e.tensor_copy(out=qnb.t[:, 0:W], in_=qn.t[:, 0:W]), r=[qn.b], w=[qnb.b])
                        ps3 = banks()
                        OP("pe", lambda e: e.matmul(ps3.t[:, 0:W], lhsT=ropeRT_bf.t[:], rhs=qnb.t[:, 0:W], start=True, stop=True),
                           r=[ropeRT_bf.b, qnb.b], w=[ps3.b])
                        OP("pool", lambda e: e.tensor_tensor(out=t1.t[:, 0:W], in0=qn.t[:, 0:W], in1=cosT.t[:, col0:col0 + W], op=ALU.mult),
                           r=[qn.b, cosT.b], w=[t1.b])
                        OP("dve", lambda e: e.tensor_tensor(out=t2.t[:, 0:W], in0=ps3.t[:, 0:W], in1=sinT.t[:, col0:col0 + W], op=ALU.mult),
                           r=[ps3.b, sinT.b], w=[t2.b])

                    for c in range(8):
                        fm_chunk(C_Q + c * 128, 0)
                        OP("pool", lambda e: e.tensor_tensor(out=qst.t[:, c, 0:W], in0=t1.t[:, 0:W], in1=t2.t[:, 0:W], op=ALU.add),
                           r=[t1.b, t2.b], w=[qst.b])
                    DMA([(x_view(qs, col0, W), qst.t[:, :, 0:W])], r=[qst.b], w=[qs_b[ti]], sync=qst.b)
                    for c in range(4):
                        fm_chunk(C_QI + c * 128, None)
                        OP("pool", lambda e: e.tensor_tensor(out=qist.t[:, c, 0:W], in0=t1.t[:, 0:W], in1=t2.t[:, 0:W], op=ALU.add),
                           r=[t1.b, t2.b], w=[qist.b])
                    DMA([(qis.rearrange("(c p) t -> p c t", p=128)[:, :, col0:col0 + W], qist.t[:, :, 0:W])], r=[qist.b], w=[qis_b[ti]], sync=qist.b)
                    for kp in range(2):
                        fm_chunk(C_K + kp * 128, 1)
                        OP("pool", lambda e: e.tensor_tensor(out=resf.t[:, 0:W], in0=t1.t[:, 0:W], in1=t2.t[:, 0:W], op=ALU.add),
                           r=[t1.b, t2.b], w=[resf.b])
                        DMA([(kT_o[j, kp * 128:(kp + 1) * 128, col0:col0 + W], resf.t[:, 0:W])], r=[resf.b], w=[out_b], sync=resf.b)
                        for hb in range(2):
                            OP("act", lambda e: e.activation(out=KTZ.t[:, 2 * kp + hb, col0:col0 + W], in_=resf.t[:, 0:W], func=AF.Identity,
                                                             scale=misc_f.t[:, hb:hb + 1]), r=[resf.b, misc_f.b], w=[KTZ.b])
                    fm_chunk(C_KI, 2)
                    OP("pool", lambda e: e.tensor_tensor(out=resf.t[:, 0:W], in0=t1.t[:, 0:W], in1=t2.t[:, 0:W], op=ALU.add),
                       r=[t1.b, t2.b], w=[resf.b])
                    DMA([(kiT_o[j, :, col0:col0 + W], resf.t[0:64, 0:W])], r=[resf.b], w=[out_b], sync=resf.b)
                    for hb in range(2):
                        OP("act", lambda e: e.activation(out=KITZ.t[:, hb, col0:col0 + W], in_=resf.t[:, 0:W], func=AF.Identity,
                                                         scale=misc_f.t[:, hb:hb + 1]), r=[resf.b, misc_f.b], w=[KITZ.b])
                    blocks = [(b_ * 128, 128) for b_ in range(W // 128)] if not is_s else [(TS * s_, TS) for s_ in range(NSMP)]
                    for bi, (b0, nb) in enumerate(blocks):
                        ps = banks()
                        for dc in range(8):
                            OP("pe", lambda e: e.matmul(ps.t[0:nb, 0:256], lhsT=h.t[:, dc, b0:b0 + nb], rhs=win.t[:, dc, C_V:C_V + 256],
                                                        start=(dc == 0), stop=(dc == 7)), r=[h.b, win.b], w=[ps.b])
                        OP("act", lambda e: e.activation(out=vout.t[0:nb, :], in_=ps.t[0:nb, 0:256], func=AF.Copy), r=[ps.b], w=[vout.b])
                        DMA([(v_o[j, col0 + b0:col0 + b0 + nb, :], vout.t[0:nb, :])], r=[vout.b], w=[out_b], sync=vout.b)
                        if not is_s:
                            blk = (col0 + b0) // 128
                            OP("pool", lambda e: e.tensor_copy(out=Vx.t[0:nb, blk, :, 0:64], in_=vout.t[0:nb, :].rearrange("p (k d) -> p k d", d=64)),
                               r=[vout.b], w=[Vx.b])
                            ps = banks()
                            for dc in range(8):
                                OP("pe", lambda e: e.matmul(ps.t[0:nb, 0:8], lhsT=h.t[:, dc, b0:b0 + nb], rhs=win.t[:, dc, C_WI:C_WI + 8],
                                                            start=(dc == 0), stop=(dc == 7)), r=[h.b, win.b], w=[ps.b])
                            OP("act", lambda e: e.activation(out=WI.t[0:nb, blk, :], in_=ps.t[0:nb, 0:8], func=AF.Copy, scale=WISCALE),
                               r=[ps.b], w=[WI.b])
                        else:
                            OP("pool", lambda e: e.tensor_copy(out=Vcur[bi].t[0:nb, :, 0:64], in_=vout.t[0:nb, :].rearrange("p (k d) -> p k d", d=64)),
                               r=[vout.b], w=[Vcur[bi].b])
                    if is_s:
                        ps = banks()
                        for dc in range(8):
                            OP("pe", lambda e: e.matmul(ps.t[0:8, 0:W], lhsT=win.t[:, dc, C_WI:C_WI + 8], rhs=h.t[:, dc, 0:W],
                                                        start=(dc == 0), stop=(dc == 7)), r=[h.b, win.b], w=[ps.b])
                        OP("act", lambda e: e.activation(out=wiT.t[:, 0:W], in_=ps.t[0:8, 0:W], func=AF.Copy, scale=WISCALE), r=[ps.b], w=[wiT.b])
                        DMA([(wsc.rearrange("s (h t) o -> h s (t o)", h=8), wiT.t[:, :].rearrange("h (s t) -> h s t", t=TS))],
                            r=[wiT.b], w=[wsc_b], sync=wiT.b)
                P.flush()
            with contextlib.ExitStack() as st:
                qblk = kb.sb(st, "qblk", [128, 8, 128], BF16)
                qiblk = kb.sb(st, "qiblk", [128, 4, 128], BF16)
                scr = kb.sb(st, "scr", [128, SEQ], F32)
                scw = kb.sb(st, "scw", [128, SEQ], F32)
                m01 = kb.sb(st, "m01", [128, SEQ], BF16)
                maskT = kb.sb(st, "maskT", [128, 32, 128], BF16)
                pT = kb.sb(st, "pT", [128, 32, 512], BF16)
                m8 = kb.sb(st, "m8", [128, 256], F32)
                rls = RR([kb.sb(st, "rl%d" % k, [128, 512], BF16) for k in range(3)])
                ets = RR([kb.sb(st, "et%d" % k, [128, 512], BF16) for k in range(2)])
                rec = kb.sb(st, "rec", [128, 4, 1], F32)
                otok = kb.sb(st, "otok", [128, 1024], BF16)
                oT = kb.sb(st, "oT", [128, 8, 128], BF16)
                banks = RR([kb.ps(st, "aps%d" % k, [128, 512], F32) for k in range(5)])
                psO = kb.ps(st, "psO", [128, 512], F32)
                psb = RR([kb.ps(st, "apsb%d" % k, [128, 512], BF16) for k in range(2)])
                nqb = int(_os.environ.get("DSA_QB", "32"))
                for qb in range(nqb):
                    col0 = qb * 128
                    ti = tile_of(col0)
                    L = col0 + 128
                    nkb = qb + 1
                    DMA([(qblk.t[:], x_view(qs, col0, 128))], r=[qs_b[ti]], w=[qblk.b])
                    DMA([(qiblk.t[:], qis.rearrange("(c p) t -> p c t", p=128)[:, :, col0:col0 + 128])], r=[qis_b[ti]], w=[qiblk.b])
                    for kt in range((L + 511) // 512):
                        k0 = kt * 512
                        kw = min(512, L - k0)
                        for ih in range(8):
                            c, hb = ih // 2, ih % 2
                            ps = banks()
                            OP("pe", lambda e: e.matmul(ps.t[:, 0:kw], lhsT=qiblk.t[:, c, :], rhs=KITZ.t[:, hb, k0:k0 + kw], start=True, stop=True),
                               r=[qiblk.b, KITZ.b], w=[ps.b])
                            rl = rls()
                            OP("act", lambda e: e.activation(out=rl.t[:, 0:kw], in_=ps.t[:, 0:kw], func=AF.Relu), r=[ps.b], w=[rl.b])
                            if ih == 0:
                                OP("dve", lambda e: e.tensor_scalar(out=scr.t[:, k0:k0 + kw], in0=rl.t[:, 0:kw], scalar1=WI.t[:, qb, ih:ih + 1],
                                                                    scalar2=None, op0=ALU.mult), r=[rl.b, WI.b], w=[scr.b])
                            else:
                                OP("dve", lambda e: e.scalar_tensor_tensor(out=scr.t[:, k0:k0 + kw], in0=rl.t[:, 0:kw], scalar=WI.t[:, qb, ih:ih + 1],
                                                                           in1=scr.t[:, k0:k0 + kw], op0=ALU.mult, op1=ALU.add),
                                   r=[rl.b, WI.b, scr.b], w=[scr.b])
                    OP("dve", lambda e: e.tensor_tensor(out=scr.t[:, col0:col0 + 128], in0=scr.t[:, col0:col0 + 128], in1=tri_f.t[:], op=ALU.add),
                       r=[scr.b, tri_f.b], w=[scr.b])
                    if qb >= 2:
                        for r_ in range(32):
                            srcw = scr if r_ == 0 else scw
                            OP("dve", lambda e: e.max(out=m8.t[:, 8 * r_:8 * r_ + 8], in_=srcw.t[:, 0:L]), r=[srcw.b], w=[m8.b])
                            if r_ < 31:
                                OP("dve", lambda e: e.match_replace(out=scw.t[:, 0:L], in_to_replace=m8.t[:, 8 * r_:8 * r_ + 8], in_values=srcw.t[:, 0:L],
                                                                    imm_value=NEG), r=[srcw.b, m8.b], w=[scw.b])
                        OP("dve", lambda e: e.tensor_scalar(out=m01.t[:, 0:L], in0=scr.t[:, 0:L], scalar1=m8.t[:, 255:256], scalar2=None, op0=ALU.is_ge),
                           r=[scr.b, m8.b], w=[m01.b])
                    else:
                        OP("dve", lambda e: e.tensor_scalar(out=m01.t[:, 0:L], in0=scr.t[:, 0:L], scalar1=-1.0e29, scalar2=None, op0=ALU.is_ge),
                           r=[scr.b], w=[m01.b])
                    for g0 in range(0, nkb, 4):
                        n = min(4, nkb - g0)
                        pb = psb()
                        for k_ in range(n):
                            OP("pe", lambda e: e.transpose(pb.t[:, k_ * 128:(k_ + 1) * 128], in_=m01.t[:, (g0 + k_) * 128:(g0 + k_ + 1) * 128],
                                                           identity=ident_bf.t[:]), r=[m01.b, ident_bf.b], w=[pb.b])
                        OP("act", lambda e: e.activation(out=maskT.t[:, g0:g0 + n, :], in_=pb.t[:, 0:n * 128].rearrange("p (k t) -> p k t", t=128),
                                                         func=AF.Copy), r=[pb.b], w=[maskT.b])
                    for kvh in range(4):
                        kp, hb = kvh // 2, kvh % 2
                        for kb_ in range(nkb):
                            ps = banks()
                            OP("pe", lambda e: e.matmul(ps.t[:, 0:512], lhsT=KTZ.t[:, kvh, kb_ * 128:(kb_ + 1) * 128],
                                                        rhs=qblk.t[:, 4 * kp:4 * kp + 4, :], start=True, stop=True), r=[KTZ.b, qblk.b], w=[ps.b])
                            et = ets()
                            OP("act", lambda e: e.activation(out=et.t[:, :], in_=ps.t[:, 0:512], func=AF.Exp, scale=QSCALE), r=[ps.b], w=[et.b])
                            OP("pool", lambda e: e.tensor_tensor(out=pT.t[:, kb_, :].rearrange("p (i t) -> p i t", t=128),
                                                                 in0=et.t[:, :].rearrange("p (i t) -> p i t", t=128),
                                                                 in1=maskT.t[:, kb_:kb_ + 1, :].to_broadcast([128, 4, 128]), op=ALU.mult),
                               r=[et.b, maskT.b], w=[pT.b])
                        for i_ in range(4):
                            for kb_ in range(nkb):
                                OP("pe", lambda e: e.matmul(psO.t[:, i_ * 65:(i_ + 1) * 65], lhsT=pT.t[:, kb_, i_ * 128:(i_ + 1) * 128],
                                                            rhs=Vx.t[:, kb_, kvh, :], start=(kb_ == 0), stop=(kb_ == nkb - 1)),
                                   r=[pT.b, Vx.b], w=[psO.b])
                        pso3 = psO.t[:, 0:260].rearrange("p (i d) -> p i d", d=65)
                        OP("dve", lambda e: e.reciprocal(out=rec.t[:], in_=pso3[:, :, 64:65]), r=[psO.b], w=[rec.b])
                        ov = otok.t[:, :].rearrange("p (c h d) -> p c h d", h=2, d=64)[:, 4 * kp:4 * kp + 4, hb, :]
                        OP("dve", lambda e: e.tensor_tensor(out=ov, in0=pso3[:, :, 0:64], in1=rec.t[:].to_broadcast([128, 4, 64]), op=ALU.mult),
                           r=[psO.b, rec.b], w=[otok.b])
                    for half in range(2):
                        pb = psb()
                        for c4 in range(4):
                            c = half * 4 + c4
                            OP("pe", lambda e: e.transpose(pb.t[:, c4 * 128:(c4 + 1) * 128], in_=otok.t[:, c * 128:(c + 1) * 128], identity=ident_bf.t[:]),
                               r=[otok.b, ident_bf.b], w=[pb.b])
                        OP("act", lambda e: e.activation(out=oT.t[:, half * 4:half * 4 + 4, :], in_=pb.t[:, 0:512].rearrange("p (c t) -> p c t", t=128),
                                                         func=AF.Copy), r=[pb.b], w=[oT.b])
                    DMA([(x_view(ms, col0, 128), oT.t[:])], r=[oT.b], w=[ms_b[ti]], sync=oT.b)
                P.flush()
            if _os.environ.get("DSA_SAMPLE", "1") == "1":
                dsa_sample(i, KTZ, KITZ, Vcur)

    def copy_stage():
        with contextlib.ExitStack() as st:
            xts = RR([kb.sb(st, "cx%d" % k, [128, 8, 512], F32) for k in range(2)])
            for (col0, W) in TILES512:
                xt = xts()
                DMA([(xt.t[:, :, 0:W], x_view(xT_in, col0, W))], w=[xt.b])
                DMA([(x_view(xs, col0, W), xt.t[:, :, 0:W])], r=[xt.b], w=[xs_b[tile_of(col0)]], sync=xt.b)
            P.flush()

    env = dict(locals())
    return env


def compose(env, mixers=(True, True, True, True), nlayers=4):
    import os
    P = env["P"]
    if os.environ.get("SKIP_FFN", "0") == "1":
        env["rwkv_stage"](0)
        return env["kb"].nc
    for i in range(nlayers):
        if mixers[i]:
            if i % 2 == 0:
                env["rwkv_stage"](i)
            else:
                env["dsa_stage"](i)
        wo = None
        if mixers[i]:
            wo = env["rw_w_o"][i // 2] if i % 2 == 0 else env["att_w_o"][i // 2]
        env["ffn_stage"](i, env["yT_o"] if i == nlayers - 1 else env["xs"], wo)
    return env["kb"].nc


def _consts():
    c = {}
    sq = np.zeros((8, 128, 128), np.float32)
    sq[0] = np.eye(128)
    sq[1] = 1.0
    sq[2, :64, :64] = 1.0
    sq[2, 64:, 64:] = 1.0
    for m in range(128):
        i = m % 64
        if i < 8:
            sq[3, m + 8, m] = -1.0
        elif i < 16:
            sq[3, m - 8, m] = 1.0
    for hb in range(2):
        for m in range(128):
            sq[4 + hb, hb * 64 + (m % 64), m] = 1.0
    tt, ss = np.meshgrid(np.arange(128), np.arange(128), indexing="ij")
    sq[6] = np.where(ss <= tt, 0.0, NEG)
    sq[7] = (tt > ss).astype(np.float32)
    c["cst_sq"] = sq
    jj, t2 = np.meshgrid(np.arange(128), np.arange(128), indexing="ij")
    strict = (jj < t2).astype(np.float32)
    incl = (jj <= t2).astype(np.float32)
    c["cst_msi"] = np.concatenate([strict, incl, strict, incl], axis=1)
    j8, t8 = np.meshgrid(np.arange(8), np.arange(8), indexing="ij")
    s8 = (j8 < t8).astype(np.float32)
    i8 = (j8 <= t8).astype(np.float32)
    c["cst_msi8"] = np.concatenate([s8, i8, s8, i8], axis=1)
    l8 = (j8 > t8).astype(np.float32)
    c["cst_ml8"] = np.concatenate([l8, l8], axis=1)
    j32, t32 = np.meshgrid(np.arange(32), np.arange(32), indexing="ij")
    s32 = (j32 < t32).astype(np.float32)
    i32 = (j32 <= t32).astype(np.float32)
    c["cst_msi32"] = np.concatenate([s32, i32, s32, i32], axis=1)
    l32 = (j32 > t32).astype(np.float32)
    c["cst_ml32"] = np.concatenate([l32, l32], axis=1)
    misc = np.zeros((128, 64), np.float32)
    misc[:64, 0] = 1.0
    misc[64:, 1] = 1.0
    for h in range(8):
        for t in range(8):
            misc[h * 8 + t, 2 + t] = 1.0
    for s in range(4):
        for t in range(8):
            for j in range(8):
                misc[s * 8 + t, 10 + j] = 0.0 if j <= t else NEG
    c["cst_misc"] = misc
    pos = np.concatenate([np.arange(SEQ), np.tile(8192 + np.arange(TS), NSMP)]).astype(np.float32)
    inv = (np.float32(500000.0) ** (-np.arange(8, dtype=np.float32) * np.float32(2.0) / np.float32(16))).astype(np.float32)
    ang = (pos[None, :] * inv[:, None]).astype(np.float32)
    cs = np.zeros((2, 128, NTOK), np.float32)
    cs[0] = 1.0
    for p in range(128):
        i = p % 64
        if i < 16:
            cs[0, p] = np.cos(ang[i % 8])
            cs[1, p] = np.sin(ang[i % 8])
    c["ropecs"] = cs
    return c


def _colT(v):
    return np.ascontiguousarray(np.asarray(v, np.float32).reshape(8, 128).T)


def prep_inputs(inp):
    f = lambda a: np.ascontiguousarray(np.asarray(a, np.float32))
    shared = dict(_consts())
    shared["w_ada"] = f(inp["w_ada"])
    shared["b_adaT"] = np.stack([np.ascontiguousarray(f(inp["b_ada"])[i].reshape(48, 128).T) for i in range(4)])
    ng = f(inp["norm_g"])
    shared["norm_gT"] = np.stack([np.stack([_colT(ng[i, w]) for w in range(2)]) for i in range(4)])
    shared["w_up"] = f(inp["w_up"])
    shared["w_down"] = f(inp["w_down"])
    for k in ("rw_w_rkv", "rw_w_o", "rw_w1", "rw_w2", "rw_a1", "rw_a2", "rw_v1", "rw_v2", "rw_g1", "rw_g2"):
        shared[k] = f(inp[k])
    rv = np.zeros((2, 128, 12, 8), np.float32)
    rr = np.zeros((2, 3, D), np.float32)
    for j in range(2):
        for n in range(6):
            rv[j, :, n, :] = _colT(inp["rw_mix"][j, n])
        rv[j, :, 6, :] = _colT(inp["rw_w0"][j])
        rv[j, :, 7, :] = _colT(inp["rw_a0"][j])
        rv[j, :, 8, :] = _colT(inp["rw_k_k"][j])
        rv[j, :, 9, :] = _colT(inp["rw_k_a"][j])
        rv[j, :, 10, :] = _colT(np.asarray(inp["rw_r_k"][j]).reshape(-1))
        rr[j, 0] = f(inp["rw_lnx_w"][j])
        rr[j, 1] = f(inp["rw_lnx_b"][j])
        if j >= 1:
            rr[j, 2] = f(inp["rw_v0"][j - 1])
    shared["rwvecT"] = rv
    shared["rwrows"] = rr
    perm = np.zeros(1024, np.int64)
    for c in range(8):
        kp, i = c // 4, c % 4
        for hb in range(2):
            head = (2 * kp + hb) * 4 + i
            perm[c * 128 + hb * 64:(c * 128 + hb * 64 + 64)] = head * 64 + np.arange(64)
    wi_ = f(inp["att_w_in"])
    w2 = np.zeros((2, D, PIN2), np.float32)
    w2[:, :, 0:1024] = wi_[:, :, 0:1024][:, :, perm]
    w2[:, :, 1024:2048] = wi_[:, :, 1024:2048]
    w2[:, :, 2048:2112] = wi_[:, :, 2048:2112]
    w2[:, :, 2112:2176] = wi_[:, :, 2048:2112]
    w2[:, :, 2176:2184] = wi_[:, :, 2112:2120]
    shared["att_w_in"] = w2
    shared["att_w_o"] = np.ascontiguousarray(f(inp["att_w_o"])[:, perm, :])
    dv = np.zeros((2, 128, 3), np.float32)
    for j in range(2):
        dv[j, :, 0] = np.tile(f(inp["att_q_norm"][j]), 2)
        dv[j, :, 1] = np.tile(f(inp["att_k_norm"][j]), 2)
        dv[j, :, 2] = np.tile(f(inp["idx_k_norm"][j]), 2)
    shared["dsavec"] = dv
    for j_ in range(2):
        shared["cache_k%d" % j_] = f(inp["cache_k"][j_]).reshape(POOLN * 16, 2048)
        shared["cache_v%d" % j_] = f(inp["cache_v"][j_]).reshape(POOLN * 16, 2048)
        shared["cache_ik%d" % j_] = f(inp["cache_idx_k"][j_]).reshape(POOLN * 4, 2048)
    xp = f(inp["x_prompt"]); xsm = f(inp["x_sample"])
    cp = f(inp["c_prompt"]); csm = f(inp["c_sample"])
    swkv = f(inp["state_wkv"]); ssh = f(inp["state_shift"])
    pt = np.asarray(inp["page_table"], np.int32)
    maps = []
    for c in range(8):
        b = c % 4
        sl = slice(4 * c, 4 * c + 4)
        m = dict(shared)
        m["xT"] = np.ascontiguousarray(np.concatenate([xp[b].T, xsm[sl].reshape(NSMP * TS, D).T], axis=1))
        m["cT"] = np.ascontiguousarray(np.concatenate([cp[b][:, None], csm[sl].T], axis=1))
        m["ptT"] = np.ascontiguousarray(pt[sl].T)
        a = swkv[:, sl].reshape(2, NSMP, 8, 2, 64, 64)
        m["wkvT0"] = np.ascontiguousarray(a.transpose(0, 1, 3, 5, 2, 4).reshape(2, NSMP, 128, 8, 64))
        sh = ssh[:, sl].reshape(2, NSMP, 8, 128)
        m["shiftT0"] = np.ascontiguousarray(sh.transpose(0, 3, 2, 1))
        maps.append(m)
    return maps


def assemble(res):
    y_p = np.zeros((4, SEQ, D), np.float32)
    y_s = np.zeros((32, TS, D), np.float32)
    k_p = np.zeros((2, 4, SEQ, 4, 64), np.float32); v_p = np.zeros_like(k_p)
    ki_p = np.zeros((2, 4, SEQ, 64), np.float32)
    wkv_p = np.zeros((2, 4, 16, 64, 64), np.float32)
    sh_p = np.zeros((2, 4, D), np.float32)
    k_s = np.zeros((2, 32, TS, 4, 64), np.float32); v_s = np.zeros_like(k_s)
    ki_s = np.zeros((2, 32, TS, 64), np.float32)
    wkv_s = np.zeros((2, 32, 16, 64, 64), np.float32)
    sh_s = np.zeros((2, 32, D), np.float32)

    def unwkv(a):
        return a.reshape(2, 64, 8, 64).transpose(2, 0, 3, 1).reshape(16, 64, 64)

    for c in range(8):
        r = res[c]
        yT = np.asarray(r["yT"]); kT = np.asarray(r["kT"]); vv = np.asarray(r["v"]); kiT = np.asarray(r["kiT"])
        wk = np.asarray(r["wkv"]); shf = np.asarray(r["shift"])
        if c < 4:
            y_p[c] = yT[:, :SEQ].T
            for j in range(2):
                k_p[j, c] = kT[j][:, :SEQ].T.reshape(SEQ, 4, 64)
                v_p[j, c] = vv[j][:SEQ].reshape(SEQ, 4, 64)
                ki_p[j, c] = kiT[j][:, :SEQ].T
                wkv_p[j, c] = unwkv(wk[j, 0])
                sh_p[j, c] = shf[j][:, :, 0].T.reshape(D)
        for s in range(NSMP):
            q = 4 * c + s
            cs_ = slice(SEQ + TS * s, SEQ + TS * s + TS)
            y_s[q] = yT[:, cs_].T
            for j in range(2):
                k_s[j, q] = kT[j][:, cs_].T.reshape(TS, 4, 64)
                v_s[j, q] = vv[j][cs_].reshape(TS, 4, 64)
                ki_s[j, q] = kiT[j][:, cs_].T
                wkv_s[j, q] = unwkv(wk[j, 1 + s])
                sh_s[j, q] = shf[j][:, :, 1 + s].T.reshape(D)
    return (y_p, y_s, k_p, v_p, ki_p, wkv_p, sh_p, k_s, v_s, ki_s, wkv_s, sh_s)


_NC_CACHE = {}


def kernel(**inputs):
    maps = prep_inputs(inputs)
    if "nc" not in _NC_CACHE:
        env = build_program()
        _NC_CACHE["nc"] = compose(env)
    nc = _NC_CACHE["nc"]
    res = run_bass_kernel_spmd(nc, maps, core_ids=list(range(8)))
    return assemble(res.results)
```
